# Optimizing a Trainium2 kernel written in Bass

```python
import jax, jax.numpy as jnp
from jax import lax
import numpy as np

D_MODEL = 2048
BATCH = 2
SEQ = 8192
DEPTH = 1

HEAD_DIM = 128
N_KV_HEADS = 4
DILATED_PATTERNS = ((128, 1), (512, 4), (2048, 16))
N_PATTERNS = len(DILATED_PATTERNS)
N_Q_HEADS = N_KV_HEADS * N_PATTERNS
Q_WIDTH = N_Q_HEADS * HEAD_DIM
KV_WIDTH = N_KV_HEADS * HEAD_DIM
ATTN_OUT_WIDTH = N_KV_HEADS * HEAD_DIM
ROT_DIMS = HEAD_DIM // 4
ROPE_THETA = 500000.0
LRU_WIDTH = D_MODEL - ATTN_OUT_WIDTH
LRU_BLOCK_WIDTH = 128
LRU_BLOCKS = LRU_WIDTH // LRU_BLOCK_WIDTH
CONV_WIDTH = 4
LRU_C = 8.0
IN_PROJ_WIDTH = Q_WIDTH + 2 * KV_WIDTH + 2 * LRU_WIDTH
D_FF = 256 * ((8 * D_MODEL // 3 + 255) // 256)
D_PLE = 256
LN_EPS = 1e-5
DEEPNORM_ALPHA = (2.0 * DEPTH) ** 0.25
DEEPNORM_BETA = (8.0 * DEPTH) ** -0.25

kernel_name = 'hybrid_rglru_dilated_attn_macaron_deepnorm'


def layer_norm(x, g, b):
    xf = x.astype(jnp.float32)
    mu = xf.mean(-1, keepdims=True)
    var = jnp.square(xf - mu).mean(-1, keepdims=True)
    y = (xf - mu) * lax.rsqrt(var + LN_EPS)
    return (y * g.astype(jnp.float32) + b.astype(jnp.float32)).astype(x.dtype)


def swiglu(x, w_gate, w_up, w_down):
    return (jax.nn.silu(x @ w_gate) * (x @ w_up)) @ w_down


def partial_rotary(t, positions):
    half = ROT_DIMS // 2
    inv_freq = jnp.power(jnp.float32(ROPE_THETA), -jnp.arange(half, dtype=jnp.float32) * (2.0 / ROT_DIMS))
    ang = positions.astype(jnp.float32)[:, None, :, None] * inv_freq
    cos = jnp.cos(ang).astype(t.dtype)
    sin = jnp.sin(ang).astype(t.dtype)
    t1 = t[..., :half]
    t2 = t[..., half:ROT_DIMS]
    return jnp.concatenate([t1 * cos - t2 * sin, t2 * cos + t1 * sin, t[..., ROT_DIMS:]], axis=-1)


def dilated_window_attention(q, k, v, window, dilation):
    b, h, s, dh = q.shape
    span = window // dilation
    blk = span
    sub_len = s // dilation
    pad = (-sub_len) % blk
    nb = (sub_len + pad) // blk

    def to_blocks(t):
        t = t.reshape(b, h, sub_len, dilation, dh).swapaxes(2, 3)
        t = jnp.pad(t, ((0, 0), (0, 0), (0, 0), (0, pad), (0, 0)))
        return t.reshape(b, h, dilation, nb, blk, dh)

    def with_prev(t):
        prev = jnp.pad(t, ((0, 0), (0, 0), (0, 0), (1, 0), (0, 0), (0, 0)))[:, :, :, :-1]
        return jnp.concatenate([prev, t], axis=4)

    qb = to_blocks(q)
    kw = with_prev(to_blocks(k))
    vw = with_prev(to_blocks(v))
    scores = jnp.einsum('bhrnqe,bhrnke->bhrnqk', qb, kw,
                        preferred_element_type=jnp.float32) * (dh ** -0.5)
    qi = jnp.arange(blk)[:, None]
    ki = jnp.arange(2 * blk)[None, :]
    dist = qi + blk - ki
    band = (dist >= 0) & (dist <= span)
    not_first = (jnp.arange(nb) > 0)[:, None, None]
    mask = band[None] & (not_first | (ki >= blk)[None])
    scores = jnp.where(mask, scores, -jnp.inf)
    m = scores.max(-1, keepdims=True)
    e = jnp.exp(scores - m)
    den = e.sum(-1, keepdims=True)
    out = jnp.einsum('bhrnqk,bhrnke->bhrnqe', e, vw.astype(jnp.float32)) / den
    lse = (m + jnp.log(den))[..., 0]
    out = out.reshape(b, h, dilation, nb * blk, dh)[:, :, :, :sub_len].swapaxes(2, 3).reshape(b, h, s, dh)
    lse = lse.reshape(b, h, dilation, nb * blk)[..., :sub_len].swapaxes(2, 3).reshape(b, h, s)
    return out, lse


def rg_lru_branch(xb, yb, conv_w, conv_b, w_rgate, b_rgate, w_igate, b_igate, lam):
    b, s, c = xb.shape
    xc = lax.conv_general_dilated(xb, conv_w[:, None, :], window_strides=(1,),
                                  padding=((CONV_WIDTH - 1, 0),),
                                  dimension_numbers=('NWC', 'WIO', 'NWC'),
                                  feature_group_count=c) + conv_b
    xh = xc.reshape(b, s, LRU_BLOCKS, LRU_BLOCK_WIDTH)
    r = jax.nn.sigmoid(jnp.einsum('bsgi,gij->bsgj', xh, w_rgate).reshape(b, s, c) + b_rgate)
    i = jax.nn.sigmoid(jnp.einsum('bsgi,gij->bsgj', xh, w_igate).reshape(b, s, c) + b_igate)
    log_a = -LRU_C * jax.nn.softplus(-lam.astype(jnp.float32)) * r.astype(jnp.float32)
    a = jnp.exp(log_a)
    u = jnp.sqrt(-jnp.expm1(2.0 * log_a)) * (i * xc).astype(jnp.float32)

    def combine(left, right):
        a1, b1 = left
        a2, b2 = right
        return a1 * a2, a2 * b1 + b2

    _, hseq = lax.associative_scan(combine, (a, u), axis=1)
    return hseq.astype(xb.dtype) * jax.nn.gelu(yb)


def hybrid_mixer(h, positions, w_in, conv_w, conv_b, w_rgate, b_rgate, w_igate, b_igate, lam, w_out):
    b, s, _ = h.shape
    proj = h @ w_in
    q, k, v, xb, yb = jnp.split(proj, [Q_WIDTH, Q_WIDTH + KV_WIDTH, Q_WIDTH + 2 * KV_WIDTH,
                                       Q_WIDTH + 2 * KV_WIDTH + LRU_WIDTH], axis=-1)
    q = q.reshape(b, s, N_PATTERNS, N_KV_HEADS, HEAD_DIM).transpose(2, 0, 3, 1, 4)
    k = k.reshape(b, s, N_KV_HEADS, HEAD_DIM).transpose(0, 2, 1, 3)
    v = v.reshape(b, s, N_KV_HEADS, HEAD_DIM).transpose(0, 2, 1, 3)
    q = partial_rotary(q, positions)
    k = partial_rotary(k, positions)
    outs = []
    lses = []
    for g, (window, dilation) in enumerate(DILATED_PATTERNS):
        o, l = dilated_window_attention(q[g], k, v, window, dilation)
        outs.append(o)
        lses.append(l)
    weights = jax.nn.softmax(jnp.stack(lses), axis=0)
    attn = jnp.einsum('gbhs,gbhse->bshe', weights, jnp.stack(outs))
    attn = attn.reshape(b, s, ATTN_OUT_WIDTH).astype(h.dtype)
    rec = rg_lru_branch(xb, yb, conv_w, conv_b, w_rgate, b_rgate, w_igate, b_igate, lam)
    return jnp.concatenate([attn, rec], axis=-1) @ w_out


def setup_inputs(seed: int = 0) -> dict:
    key = jax.random.key(seed)
    ks = jax.random.split(key, 32)
    f32 = jnp.float32

    def nrm(k, shape, scale):
        return jax.random.normal(k, shape, f32) * scale

    x = jax.random.normal(ks[0], (BATCH, SEQ, D_MODEL), f32)
    p = jax.random.normal(ks[1], (DEPTH, BATCH, SEQ, D_PLE), f32)
    offset = jax.random.randint(ks[2], (BATCH, 1), 0, 1024, dtype=jnp.int32)
    positions = (jnp.arange(SEQ, dtype=jnp.int32)[None, :] + offset).astype(jnp.int32)
    a_pow = jax.random.uniform(ks[3], (DEPTH, LRU_WIDTH), f32, minval=0.9, maxval=0.999)
    a_base = a_pow ** (1.0 / LRU_C)
    lru_lambda = jnp.log(a_base) - jnp.log1p(-a_base)
    return {
        'x': x,
        'p': p,
        'positions': positions,
        'ffn1_w_gate': nrm(ks[4], (DEPTH, D_MODEL, D_FF), D_MODEL ** -0.5),
        'ffn1_w_up': nrm(ks[5], (DEPTH, D_MODEL, D_FF), D_MODEL ** -0.5),
        'ffn1_w_down': nrm(ks[6], (DEPTH, D_FF, D_MODEL), D_FF ** -0.5 * DEEPNORM_BETA),
        'ln1_g': 1.0 + nrm(ks[7], (DEPTH, D_MODEL), 0.02),
        'ln1_b': nrm(ks[8], (DEPTH, D_MODEL), 0.02),
        'w_in': nrm(ks[9], (DEPTH, D_MODEL, IN_PROJ_WIDTH), D_MODEL ** -0.5),
        'conv_w': nrm(ks[10], (DEPTH, CONV_WIDTH, LRU_WIDTH), CONV_WIDTH ** -0.5),
        'conv_b': nrm(ks[11], (DEPTH, LRU_WIDTH), 0.02),
        'w_rgate': nrm(ks[12], (DEPTH, LRU_BLOCKS, LRU_BLOCK_WIDTH, LRU_BLOCK_WIDTH), LRU_BLOCK_WIDTH ** -0.5),
        'b_rgate': nrm(ks[13], (DEPTH, LRU_WIDTH), 0.02),
        'w_igate': nrm(ks[14], (DEPTH, LRU_BLOCKS, LRU_BLOCK_WIDTH, LRU_BLOCK_WIDTH), LRU_BLOCK_WIDTH ** -0.5),
        'b_igate': nrm(ks[15], (DEPTH, LRU_WIDTH), 0.02),
        'lru_lambda': lru_lambda,
        'w_out': nrm(ks[16], (DEPTH, D_MODEL, D_MODEL), D_MODEL ** -0.5 * DEEPNORM_BETA),
        'ln2_g': 1.0 + nrm(ks[17], (DEPTH, D_MODEL), 0.02),
        'ln2_b': nrm(ks[18], (DEPTH, D_MODEL), 0.02),
        'ffn2_w_gate': nrm(ks[19], (DEPTH, D_MODEL, D_FF), D_MODEL ** -0.5),
        'ffn2_w_up': nrm(ks[20], (DEPTH, D_MODEL, D_FF), D_MODEL ** -0.5),
        'ffn2_w_down': nrm(ks[21], (DEPTH, D_FF, D_MODEL), D_FF ** -0.5 * DEEPNORM_BETA),
        'ln3_g': 1.0 + nrm(ks[22], (DEPTH, D_MODEL), 0.02),
        'ln3_b': nrm(ks[23], (DEPTH, D_MODEL), 0.02),
        'w_ple_proj': nrm(ks[24], (DEPTH, D_PLE, D_MODEL), D_PLE ** -0.5),
        'w_ple_gate': nrm(ks[25], (DEPTH, D_MODEL, D_MODEL), D_MODEL ** -0.5),
    }


def reference(x, p, positions, ffn1_w_gate, ffn1_w_up, ffn1_w_down, ln1_g, ln1_b,
              w_in, conv_w, conv_b, w_rgate, b_rgate, w_igate, b_igate, lru_lambda, w_out,
              ln2_g, ln2_b, ffn2_w_gate, ffn2_w_up, ffn2_w_down, ln3_g, ln3_b,
              w_ple_proj, w_ple_gate):
    for i in range(DEPTH):
        x = layer_norm(DEEPNORM_ALPHA * x + 0.5 * swiglu(x, ffn1_w_gate[i], ffn1_w_up[i], ffn1_w_down[i]),
                       ln1_g[i], ln1_b[i])
        mix = hybrid_mixer(x, positions, w_in[i], conv_w[i], conv_b[i], w_rgate[i], b_rgate[i],
                           w_igate[i], b_igate[i], lru_lambda[i], w_out[i])
        x = layer_norm(DEEPNORM_ALPHA * x + mix, ln2_g[i], ln2_b[i])
        x = layer_norm(DEEPNORM_ALPHA * x + 0.5 * swiglu(x, ffn2_w_gate[i], ffn2_w_up[i], ffn2_w_down[i]),
                       ln3_g[i], ln3_b[i])
        x = x + jax.nn.sigmoid(x @ w_ple_gate[i]) * (p[i] @ w_ple_proj[i])
    return x
```

```python
import contextlib
import numpy as np
import concourse.bass as bass
import concourse.mybir as mybir
from concourse.bass_utils import run_bass_kernel_spmd

F32 = mybir.dt.float32
BF16 = mybir.dt.bfloat16
I32 = mybir.dt.int32
AF = mybir.ActivationFunctionType
ALU = mybir.AluOpType
AX = mybir.AxisListType

NCORES = 8
D = 2048
DFF = 5632
NPROJ = 5632
SEQ = 8192
BATCH = 2
TOK = 2048
TP = 1024
NPASS = TOK // TP
ALPHA = 2.0 ** 0.25
EPS = 1e-5
KD = D // 128
KF = DFF // 128


class View:
    __slots__ = ("buf", "ap")

    def __init__(self, buf, ap):
        self.buf = buf
        self.ap = ap


class Buf:
    def __init__(self, name, t):
        self.name = name
        self.t = t
        self.w = {}
        self.r = {}
        self.dkey = None

    def __getitem__(self, idx):
        return View(self, self.t[idx])

    def v(self, ap):
        return View(self, ap)


class FW:
    def __init__(self, nc, stack):
        self.nc = nc
        self.stack = stack
        self.alloc = stack
        self.eng = {"pe": nc.tensor, "act": nc.scalar, "dve": nc.vector, "pool": nc.gpsimd, "sp": nc.sync}
        self.sems = {}
        self.cnt = {}
        self.known = {e: {} for e in self.eng}
        for e in ("pe", "act", "dve", "pool"):
            self._newsem(e)

    def _newsem(self, key):
        self.sems[key] = self.stack.enter_context(self.nc.semaphore("s_" + key))
        self.cnt[key] = 0

    def sb(self, name, shape, dt):
        return self.alloc.enter_context(self.nc.sbuf_tensor(name, list(shape), dt))

    def ps(self, name, shape, dt):
        return self.alloc.enter_context(self.nc.psum_tensor(name, list(shape), dt))

    def dram(self, name, shape, dt, kind):
        return self.nc.dram_tensor(name, list(shape), dt, kind=kind)

    def _wait(self, e, needs):
        for key, val in needs.items():
            if e == "pe" and key == "pe":
                continue
            if self.known[e].get(key, 0) < val:
                self.eng[e].wait_ge(self.sems[key], val)
                self.known[e][key] = val

    @staticmethod
    def _needs(outs, ins):
        needs = {}

        def need(d):
            for k, v in d.items():
                if needs.get(k, 0) < v:
                    needs[k] = v

        for v in ins:
            need(v.buf.w)
        for v in outs:
            need(v.buf.w)
            need(v.buf.r)
        return needs

    def op(self, e, fn, outs, ins):
        self._wait(e, self._needs(outs, ins))
        inst = fn(self.eng[e])
        self.cnt[e] += 1
        c = self.cnt[e]
        inst.then_inc(self.sems[e], 1)
        for v in ins:
            v.buf.r[e] = c
        for v in outs:
            v.buf.w[e] = c
            v.buf.r = {}
        return inst

    def dma(self, q, out, in_, cont=False, **kw):
        needs = self._needs([out], [in_])
        if cont and out.buf.dkey is not None:
            needs.pop(out.buf.dkey, None)
        self._wait(q, needs)
        inst = self.eng[q].dma_start(out=out.ap, in_=in_.ap, **kw)
        key = out.buf.dkey
        if key is None:
            key = "d_" + out.buf.name
            out.buf.dkey = key
            self._newsem(key)
        self.cnt[key] += 16
        c = self.cnt[key]
        inst.then_inc(self.sems[key], 16)
        in_.buf.r[key] = c
        out.buf.w[key] = c
        out.buf.r = {}
        return inst

    def allgather(self, out, in_):
        self._wait("pool", self._needs([out[:, :]], [in_[:, :]]))
        inst = self.eng["pool"].collective_compute(
            "AllGather", ALU.bypass, replica_groups=[[0, 1, 2, 3], [4, 5, 6, 7]],
            ins=[in_.t.ap().opt()], outs=[out.t.ap().opt()])
        key = "c_" + out.name
        self._newsem(key)
        self.cnt[key] += 1
        inst.then_inc(self.sems[key])
        in_.r[key] = 1
        out.w[key] = 1
        out.r = {}

    def barrier(self):
        for e in ("pe", "act", "dve", "pool", "sp"):
            self._wait(e, dict(self.cnt))

    def finish(self, bufs, e="sp"):
        needs = {}
        for b in bufs:
            for k, v in b.w.items():
                needs[k] = max(needs.get(k, 0), v)
        self._wait(e, needs)


def make_ident(fw, ident):
    fw.op("pool", lambda g: g.memset(ident[:, :].ap, 1.0), [ident[:, :]], [])
    fw.op("pool", lambda g: g.affine_select(out=ident[:, :].ap, in_=ident[:, :].ap, pattern=[[-1, 128]],
                                            compare_op=ALU.is_equal, fill=0.0, base=0, channel_multiplier=1),
          [ident[:, :]], [ident[:, :]])


class Dense:
    def __init__(self, fw, nslot, banks, pfx):
        self.fw = fw
        self.pfx = pfx
        self.NT = TP // 128
        self.NH = TP // 512
        self.ident = Buf(pfx + "ident", fw.sb(pfx + "ident", [128, 128], F32))
        make_ident(fw, self.ident)
        xin_t = fw.sb(pfx + "xin", [128, self.NT, D], F32)
        self.xin = [Buf(f"{pfx}xin{t}", xin_t) for t in range(self.NT)]
        self.xT = Buf(pfx + "xT", fw.sb(pfx + "xT", [128, KD, TP], BF16))
        self.HG = 12
        hT_t = fw.sb(pfx + "hT", [128, self.HG, TP], BF16)
        self.hT = [Buf(f"{pfx}hT{j}", hT_t) for j in range(self.HG)]
        self.NSLOT = nslot
        self.wt = fw.sb(pfx + "wbuf", [128, nslot, 8192], BF16)
        self.wslot = [Buf(f"{pfx}w{s}", self.wt) for s in range(nslot)]
        self.wi = 0
        self.pref = {}
        self.gsb = [Buf(f"{pfx}gsb{i}", fw.sb(f"{pfx}gsb{i}", [128, 512], F32)) for i in range(2)]
        stat_t = fw.sb(pfx + "stat", [128, self.NT, 24], F32)
        mv_t = fw.sb(pfx + "mv", [128, self.NT, 8], F32)
        self.stat = [Buf(f"{pfx}stat{t}", stat_t) for t in range(self.NT)]
        self.mv = [Buf(f"{pfx}mv{t}", mv_t) for t in range(self.NT)]
        self.gB = Buf(pfx + "gB", fw.sb(pfx + "gB", [128, D], F32))
        self.bB = Buf(pfx + "bB", fw.sb(pfx + "bB", [128, D], F32))
        self.gbc = Buf(pfx + "gbc", fw.sb(pfx + "gbc", [128, 2, KD], F32))
        self.gsrc = Buf(pfx + "gsrc", fw.sb(pfx + "gsrc", [KD, 2, 128], F32))
        self.bank = banks
        self.gi = 0
        self.ti = 0

    def prefetch(self, specs):
        for (wdram, row0, nk, col0) in specs:
            self.pref[(wdram.name, row0, nk, col0)] = self.load_w(wdram, row0, nk, col0)

    def load_w(self, wdram, row0, nk, col0, ncols=512):
        key = (wdram.name, row0, nk, col0)
        if key in self.pref:
            return self.pref.pop(key)
        s = self.wi % self.NSLOT
        self.wi += 1
        slot = self.wslot[s]
        src = wdram.t[row0:row0 + nk * 128, col0:col0 + ncols].rearrange("(k p) n -> p k n", p=128)
        dst = self.wt[:, s, 0:nk * ncols].rearrange("p (k n) -> p k n", n=ncols)
        step = 4
        for k0 in range(0, nk, step):
            k1 = min(nk, k0 + step)
            self.fw.dma("pool", slot.v(dst[:, k0:k1, :]), wdram.v(src[:, k0:k1, :]), cont=(k0 > 0))
        return slot, dst

    def load_gb(self, gvec, bvec):
        self.fw.dma("sp", self.gB[:, :], gvec.v(gvec.t[0:1, :].partition_broadcast(128)))
        self.fw.dma("sp", self.bB[:, :], bvec.v(bvec.t[0:1, :].partition_broadcast(128)))
        fw = self.fw
        fw.dma("sp", self.gsrc[:, 0, :], gvec.v(gvec.t[0, :].rearrange("(k p) -> k p", p=128)))
        fw.dma("sp", self.gsrc[:, 1, :], bvec.v(bvec.t[0, :].rearrange("(k p) -> k p", p=128)), cont=True)
        bk = self.bank[self.ti % 4]
        self.ti += 1
        for q in range(2):
            fw.op("pe", lambda pe, q=q: pe.transpose(out=bk.t[:, q * KD:(q + 1) * KD], in_=self.gsrc.t[:, q, :],
                                                     identity=self.ident.t[0:KD, 0:KD]),
                  [bk[:, :]], [self.gsrc[:, :, :], self.ident[:, :]])
        fw.op("dve", lambda v: v.tensor_copy(out=self.gbc.t[:, :, :].rearrange("p a k -> p (a k)"), in_=bk.t[:, 0:2 * KD]),
              [self.gbc[:, :, :]], [bk[:, :]])

    def to_feature_major(self, affine=False):
        fw = self.fw
        for k in range(KD):
            for hf in range(self.NH):
                bk = self.bank[self.ti % 4]
                self.ti += 1
                for i in range(4):
                    t = hf * 4 + i
                    fw.op("pe", lambda pe, i=i, t=t, k=k, bk=bk: pe.transpose(
                        out=bk.t[:, i * 128:(i + 1) * 128], in_=self.xin[t].t[:, t, k * 128:(k + 1) * 128],
                        identity=self.ident.t[:, :]), [bk[:, :]], [self.xin[t][:, t, :], self.ident[:, :]])
                dst = self.xT.t[:, k, hf * 512:(hf + 1) * 512]
                if affine:
                    if self.ti % 2 == 0:
                        fw.op("dve", lambda v, bk=bk, dst=dst, k=k: v.tensor_scalar(
                            out=dst, in0=bk.t[:, :], scalar1=self.gbc.t[:, 0, k:k + 1], scalar2=self.gbc.t[:, 1, k:k + 1],
                            op0=ALU.mult, op1=ALU.add), [self.xT[:, :, :]], [bk[:, :], self.gbc[:, :, :]])
                    else:
                        fw.op("act", lambda a, bk=bk, dst=dst, k=k: a.activation(
                            out=dst, in_=bk.t[:, :], func=AF.Identity, scale=self.gbc.t[:, 0, k:k + 1],
                            bias=self.gbc.t[:, 1, k:k + 1]), [self.xT[:, :, :]], [bk[:, :], self.gbc[:, :, :]])
                elif self.ti % 2 == 0:
                    fw.op("dve", lambda v, bk=bk, dst=dst: v.tensor_copy(out=dst, in_=bk.t[:, :]), [self.xT[:, :, :]], [bk[:, :]])
                else:
                    fw.op("act", lambda a, bk=bk, dst=dst: a.activation(out=dst, in_=bk.t[:, :], func=AF.Copy),
                          [self.xT[:, :, :]], [bk[:, :]])

    def gate_up(self, wg, wu, j0, j1):
        fw = self.fw
        for j4 in range(j0 // 4, j1 // 4):
            sg, ag = self.load_w(wg, 0, KD, j4 * 512)
            su, au = self.load_w(wu, 0, KD, j4 * 512)
            for jj in range(4):
                jl = j4 * 4 + jj - j0
                bset = 4 * (self.gi % 2)
                self.gi += 1
                for (sl, a, off) in ((sg, ag, 0), (su, au, 2)):
                    for hf in range(self.NH):
                        bk = self.bank[bset + off + hf]
                        for k in range(KD):
                            fw.op("pe", lambda pe, k=k, jj=jj, bk=bk, a=a, hf=hf: pe.matmul(
                                bk.t[:, :], lhsT=a[:, k, jj * 128:(jj + 1) * 128], rhs=self.xT.t[:, k, hf * 512:(hf + 1) * 512],
                                start=(k == 0), stop=(k == KD - 1)), [bk[:, :]], [sl[:, :, :], self.xT[:, :, :]])
                hb = self.hT[jl]
                for hf in range(self.NH):
                    bg = self.bank[bset + hf]
                    bu = self.bank[bset + 2 + hf]
                    gs = self.gsb[hf % 2]
                    fw.op("act", lambda a_, bg=bg, gs=gs: a_.activation(out=gs.t[:, :], in_=bg.t[:, :], func=AF.Silu),
                          [gs[:, :]], [bg[:, :]])
                    fw.op("dve", lambda v, bu=bu, gs=gs, jl=jl, hb=hb, hf=hf: v.tensor_tensor(
                        out=hb.t[:, jl, hf * 512:(hf + 1) * 512], in0=bu.t[:, :], in1=gs.t[:, :], op=ALU.mult),
                        [hb[:, jl, :]], [bu[:, :], gs[:, :]])

    def proj_tm(self, w, row0, nk, lhs_of, lhs_bufs, scale):
        fw = self.fw
        for g in range(D // 512):
            sl, a = self.load_w(w, row0, nk, g * 512)
            for t in range(self.NT):
                bk = self.bank[t % 8]
                for kk in range(nk):
                    fw.op("pe", lambda pe, kk=kk, t=t, bk=bk, a=a: pe.matmul(
                        bk.t[:, :], lhsT=lhs_of(kk, t), rhs=a[:, kk, :], start=(kk == 0), stop=(kk == nk - 1)),
                        [bk[:, :]], [sl[:, :, :]] + lhs_bufs(kk))
            for t in range(self.NT):
                bk = self.bank[t % 8]
                xs = self.xin[t].t[:, t, g * 512:(g + 1) * 512]
                fw.op("dve", lambda v, bk=bk, xs=xs: v.scalar_tensor_tensor(
                    out=xs, in0=bk.t[:, :], scalar=float(scale), in1=xs, op0=ALU.mult, op1=ALU.add),
                    [self.xin[t][:, t, :]], [bk[:, :], self.xin[t][:, t, :]])

    def ffn(self, wg, wu, wd, prescale=None):
        for j0 in range(0, KF, self.HG):
            j1 = min(KF, j0 + self.HG)
            self.gate_up(wg, wu, j0, j1)
            if j0 == 0 and prescale is not None:
                self.scale_x(prescale)
            self.proj_tm(wd, j0 * 128, j1 - j0, lambda kk, t: self.hT[kk].t[:, kk, t * 128:(t + 1) * 128],
                         lambda kk: [self.hT[kk][:, kk, :]], 0.5)

    def layer_norm(self):
        fw = self.fw
        gB, bB = self.gB, self.bB

        def front(i):
            X = self.xin[i]
            xt = X.t[:, i, :]
            XV = [X[:, i, :]]
            st, mv = self.stat[i], self.mv[i]
            ST = [st[:, i, :]]
            MV = [mv[:, i, :]]
            m = lambda a, b: mv.t[:, i, a:b]
            for c in range(4):
                fw.op("dve", lambda v, c=c: v.bn_stats(out=st.t[:, i, c * 6:(c + 1) * 6], in_=X.t[:, i, c * 512:(c + 1) * 512]),
                      ST, XV)
            fw.op("dve", lambda v: v.bn_aggr(out=m(0, 2), in_=st.t[:, i, :]), MV, ST)
            fw.op("dve", lambda v: v.tensor_scalar_add(out=m(2, 3), in0=m(1, 2), scalar1=EPS), MV, MV)
            fw.op("act", lambda a: a.activation(out=m(3, 4), in_=m(2, 3), func=AF.Sqrt), MV, MV)
            fw.op("dve", lambda v: v.reciprocal(out=m(4, 5), in_=m(3, 4)), MV, MV)
            fw.op("dve", lambda v: v.scalar_tensor_tensor(out=m(5, 6), in0=m(0, 1), scalar=-1.0, in1=m(4, 5),
                                                          op0=ALU.mult, op1=ALU.mult), MV, MV)
            fw.op("act", lambda a: a.activation(out=xt, in_=xt, func=AF.Identity, bias=m(5, 6), scale=m(4, 5)), XV, XV + MV)

        def back(i):
            X = self.xin[i]
            xt = X.t[:, i, :]
            XV = [X[:, i, :]]
            fw.op("dve", lambda v: v.tensor_tensor(out=xt, in0=xt, in1=gB.t[:, :], op=ALU.mult), XV, XV + [gB[:, :]])
            if i % 2 == 0:
                fw.op("pool", lambda g: g.tensor_tensor(out=xt, in0=xt, in1=bB.t[:, :], op=ALU.add), XV, XV + [bB[:, :]])
            else:
                fw.op("dve", lambda v: v.tensor_tensor(out=xt, in0=xt, in1=bB.t[:, :], op=ALU.add), XV, XV + [bB[:, :]])

        for i in range(self.NT):
            front(i)
        self._ln_back = back

    def layer_norm_finish(self):
        for i in range(self.NT):
            self._ln_back(i)

    def load_x(self, xd, tok0):
        for t in range(self.NT):
            self.fw.dma("sp", self.xin[t][:, t, :], xd.v(xd.t[tok0 + t * 128:tok0 + (t + 1) * 128, :]))

    def scale_x(self, s):
        for i in range(self.NT):
            self.fw.op("act", lambda a, i=i: a.mul(out=self.xin[i].t[:, i, :], in_=self.xin[i].t[:, i, :], mul=float(s)),
                       [self.xin[i][:, i, :]], [self.xin[i][:, i, :]])

    def store_x(self, od, tok0):
        for t in range(self.NT):
            self.fw.dma("sp", od.v(od.t[tok0 + t * 128:tok0 + (t + 1) * 128, :]), self.xin[t][:, t, :], cont=(t > 0))
        for t in range(self.NT):
            self.xin[t].r[od.dkey] = self.fw.cnt[od.dkey]


PATTERNS = ((128, 1), (512, 4), (2048, 16))
CH = 2048
NCH = SEQ // CH
SC = 128.0 ** -0.5
PI = 3.141592653589793
C1 = 6.28125
C2 = 2 * PI - C1
PI_SAFE = 3.1415925


NBLK = 12


def build_fused():
    nc = bass.Bass("TRN2", target_bir_lowering=False)
    with contextlib.ExitStack() as stack:
        fw = FW(nc, stack)
        mk = lambda n, sh, kind="ExternalInput", dt=F32: Buf(n, fw.dram(n, sh, dt, kind))
        itn = lambda n, sh, dt=F32: Buf(n, fw.dram(n, sh, dt, "Internal"))
        x = mk("x", [TOK, D]); pT = mk("pT", [256, TOK]); posd = mk("pos", [1, 2 * TOK], dt=I32)
        flagd = mk("flag", [128, 1]); mskd = mk("msk", [128, 4]); cstd = mk("cst", [32, 2])
        wg1 = mk("wg1", [D, DFF]); wu1 = mk("wu1", [D, DFF]); wd1 = mk("wd1", [DFF, D])
        g1 = mk("g1", [1, D]); b1 = mk("b1", [1, D]); win = mk("win", [D, NPROJ])
        pard = mk("par", [NBLK, 128, 8]); wrd = mk("wr", [NBLK, 128, 128]); wid = mk("wi", [NBLK, 128, 128])
        wout = mk("wout", [D, D]); g2 = mk("g2", [1, D]); b2 = mk("b2", [1, D])
        wg2 = mk("wg2", [D, DFF]); wu2 = mk("wu2", [D, DFF]); wd2 = mk("wd2", [DFF, D])
        g3 = mk("g3", [1, D]); b3 = mk("b3", [1, D]); wpg = mk("wpg", [D, D]); wpp = mk("wpp", [256, D])
        out = mk("out", [TOK, D], "ExternalOutput")
        x1s = itn("x1s", [TOK, D]); projS = itn("projS", [NPROJ, TOK]); mTs = itn("mTs", [D, TOK])
        kx = [itn(f"kx{h}", [128, TOK]) for h in range(4)]; vx = [itn(f"vx{i}", [512, 512]) for i in range(4)]
        gK = [itn(f"gK{h}", [512, TOK]) for h in range(4)]; gV = [itn(f"gV{i}", [2048, 512]) for i in range(4)]
        Vall = itn("Vall", [2 * TOK, 512])
        xbt = itn("xbt", [128, 36]); gX = itn("gX", [512, 36]); stx = itn("stx", [128, 24]); gS = itn("gS", [512, 24])
        P1t = fw.dram("P1s", [NBLK, 128, TOK], F32, "Internal"); P2t = fw.dram("P2s", [NBLK, 128, TOK], F32, "Internal")
        _p1 = [Buf(f"P1s{q}", P1t) for q in range(2)]; _p2 = [Buf(f"P2s{q}", P2t) for q in range(2)]
        P1s = [_p1[j % 2] for j in range(NBLK)]; P2s = [_p2[j % 2] for j in range(NBLK)]
        stack.enter_context(nc.Block())
        bank = [Buf(f"bank{i}", fw.ps(f"bank{i}", [128, 512], F32)) for i in range(8)]
        pid = nc.gpsimd.partition_id()
        prev = (pid + 3) % 4
        T = lambda n, sh=(128, CH), dt=F32: Buf(n, fw.sb(n, list(sh), dt))
        V = lambda b, ap: b.v(ap)

        with contextlib.ExitStack() as sub:
            fw.alloc = sub
            dn = Dense(fw, nslot=3, banks=bank, pfx="a_")
            pst = [Buf(f"pst{i}", fw.sb(f"pst{i}", [128, 512], F32)) for i in range(4)]
            pi = 0
            dn.load_gb(g1, b1)
            dn.load_x(x, 0)
            for p in range(NPASS):
                tok0 = p * TP
                dn.to_feature_major()
                dn.scale_x(ALPHA)
                dn.ffn(wg1, wu1, wd1)
                dn.prefetch([(win, 0, KD, 0), (win, 0, KD, 512)])
                dn.layer_norm()
                dn.to_feature_major(affine=True)
                dn.layer_norm_finish()
                for c4 in range(NPROJ // 512):
                    if c4 == 2:
                        dn.store_x(x1s, tok0)
                    if c4 == 6 and p + 1 < NPASS:
                        dn.load_x(x, tok0 + TP)
                    sl, a = dn.load_w(win, 0, KD, c4 * 512)
                    units = [(cc, hf) for cc in range(4) for hf in range(dn.NH)] if c4 != 4 else [(t, None) for t in range(dn.NT)]
                    for (u0, u1) in units:
                        bk = dn.bank[dn.ti % 8]
                        dn.ti += 1
                        st = pst[pi % 4]
                        pi += 1
                        if c4 == 4:
                            t = u0
                            for k in range(KD):
                                fw.op("pe", lambda pe, k=k, t=t, bk=bk, a=a: pe.matmul(
                                    bk.t[:, :], lhsT=dn.xT.t[:, k, t * 128:(t + 1) * 128], rhs=a[:, k, :],
                                    start=(k == 0), stop=(k == KD - 1)), [bk[:, :]], [sl[:, :, :], dn.xT[:, :, :]])
                        else:
                            cc, hf = u0, u1
                            for k in range(KD):
                                fw.op("pe", lambda pe, k=k, cc=cc, hf=hf, bk=bk, a=a: pe.matmul(
                                    bk.t[:, :], lhsT=a[:, k, cc * 128:(cc + 1) * 128], rhs=dn.xT.t[:, k, hf * 512:(hf + 1) * 512],
                                    start=(k == 0), stop=(k == KD - 1)), [bk[:, :]], [sl[:, :, :], dn.xT[:, :, :]])
                        if pi % 2 == 0:
                            fw.op("act", lambda a_, st=st, bk=bk: a_.activation(out=st.t[:, :], in_=bk.t[:, :], func=AF.Copy),
                                  [st[:, :]], [bk[:, :]])
                        else:
                            fw.op("dve", lambda v, st=st, bk=bk: v.tensor_copy(out=st.t[:, :], in_=bk.t[:, :]),
                                  [st[:, :]], [bk[:, :]])
                        if c4 == 4:
                            tg = tok0 + u0 * 128
                            vi, vr = tg // 512, tg % 512
                            fw.dma("sp", vx[vi].v(vx[vi].t[vr:vr + 128, :]), st[:, :])
                        else:
                            c = c4 * 4 + u0
                            cs_ = slice(tok0 + u1 * 512, tok0 + (u1 + 1) * 512)
                            if 12 <= c < 16:
                                fw.dma("sp", kx[c - 12].v(kx[c - 12].t[:, cs_]), st[:, :])
                            else:
                                fw.dma("sp", projS.v(projS.t[c * 128:(c + 1) * 128, cs_]), st[:, :])
            fw.barrier()
        fw.alloc = stack

        fw.dma("sp", xbt.v(xbt.t[:, :].rearrange("p (j s) -> p j s", s=3)),
               projS.v(projS.t[2560:2560 + 1536, TOK - 3:TOK].rearrange("(j p) s -> p j s", p=128)))
        fw.allgather(gX, xbt)
        for h in range(4):
            fw.allgather(gK[h], kx[h])
        for i in range(4):
            fw.allgather(gV[i], vx[i])

        with contextlib.ExitStack() as sub:
            fw.alloc = sub
            SETS = []
            for q_ in range(2):
                SETS.append(dict(xb=T(f"xb_t{q_}", (128, CH + 3)), yb=T(f"yb_t{q_}"), xc=T(f"xc{q_}"), xcb=T(f"xcb{q_}", dt=BF16),
                                 r=T(f"r_t{q_}"), i=T(f"i_t{q_}"), s=T(f"s_t{q_}"), h=T(f"h_t{q_}"), A=T(f"A_t{q_}"),
                                 t1=T(f"t1{q_}"), t2=T(f"t2{q_}")))
            zer = T("zer")
            par = T("par_sb", (128, NBLK, 8)); wrb = T("wrb", (128, NBLK, 128), BF16); wib = T("wib", (128, NBLK, 128), BF16)
            sc = T("sc", (128, 16)); cc_ = T("cc", (128, NBLK, 2)); st_sb = T("st_sb", (128, 24)); xh = T("xh", (128, 36))
            flag = T("flag_sb", (128, 1))
            fw.dma("sp", par[:, :, :], pard.v(pard.t[:, :, :].rearrange("j p c -> p j c")))
            fw.dma("pool", wrb[:, :, :], wrd.v(wrd.t[:, :, :].rearrange("j p c -> p j c")))
            fw.dma("pool", wib[:, :, :], wid.v(wid.t[:, :, :].rearrange("j p c -> p j c")))
            fw.dma("sp", flag[:, :], flagd[:, :])
            fw.dma("pool", xh[:, :], gX.v(gX.t[bass.ds(prev * 128, 128), :]))
            fw.op("dve", lambda v: v.tensor_scalar_mul(out=xh.t[:, :], in0=xh.t[:, :], scalar1=flag.t[:, 0:1]),
                  [xh[:, :]], [xh[:, :], flag[:, :]])
            fw.op("pool", lambda g: g.memset(zer.t[:, :], 0.0), [zer[:, :]], [])
            col = lambda i: sc.t[:, i:i + 1]
            S_ = [sc[:, :]]

            def dv(fn):
                fw.op("dve", fn, S_, S_ + [par[:, :, :]])

            for j in range(NBLK):
                lam = par.t[:, j, 7:8]
                dv(lambda v: v.tensor_scalar_mul(out=col(0), in0=lam, scalar1=-1.0))
                dv(lambda v: v.tensor_tensor(out=col(1), in0=col(0), in1=lam, op=ALU.max))
                fw.op("act", lambda a: a.activation(out=col(2), in_=col(1), func=AF.Exp, scale=-1.0), S_, S_)
                dv(lambda v: v.tensor_scalar_add(out=col(3), in0=col(2), scalar1=2.0))
                dv(lambda v: v.reciprocal(out=col(4), in_=col(3)))
                dv(lambda v: v.tensor_tensor(out=col(5), in0=col(2), in1=col(4), op=ALU.mult))
                dv(lambda v: v.tensor_tensor(out=col(6), in0=col(5), in1=col(5), op=ALU.mult))
                dv(lambda v: v.tensor_scalar(out=col(7), in0=col(6), scalar1=1.0 / 13, scalar2=1.0 / 11, op0=ALU.mult, op1=ALU.add))
                for cst in (1.0 / 9, 1.0 / 7, 1.0 / 5, 1.0 / 3, 1.0):
                    dv(lambda v, cst=cst: v.tensor_scalar(out=col(7), in0=col(7), scalar1=col(6), scalar2=cst,
                                                          op0=ALU.mult, op1=ALU.add))
                dv(lambda v: v.tensor_scalar(out=col(8), in0=col(7), scalar1=col(5), scalar2=2.0, op0=ALU.mult, op1=ALU.mult))
                dv(lambda v: v.tensor_scalar_max(out=col(9), in0=col(0), scalar1=0.0))
                dv(lambda v: v.tensor_tensor(out=col(10), in0=col(8), in1=col(9), op=ALU.add))
                fw.op("dve", lambda v, j=j: v.tensor_scalar_mul(out=cc_.t[:, j, 0:1], in0=col(10), scalar1=-8.0), [cc_[:, :, :]], S_)
                fw.op("dve", lambda v, j=j: v.tensor_scalar_mul(out=cc_.t[:, j, 1:2], in0=col(10), scalar1=-16.0), [cc_[:, :, :]], S_)

            bi_ = 0
            onec = T("onec", (128, 1))
            fw.op("pool", lambda g: g.memset(onec.t[:, :], 1.0), [onec[:, :]], [])

            def lru_loads(j):
                Z = SETS[j % 2]
                fw.dma("sp", V(Z["xb"], Z["xb"].t[:, 3:CH + 3]), projS.v(projS.t[2560 + j * 128:2560 + (j + 1) * 128, :]))
                fw.dma("sp", Z["yb"][:, :], projS.v(projS.t[4096 + j * 128:4096 + (j + 1) * 128, :]))

            lru_loads(0)

            def stage_a(j):
                nonlocal bi_
                Z = SETS[j % 2]
                xb_t, yb_t, xc, xcb, r_t, i_t, s_t, h_t, A_t, t1, t2 = (Z[k] for k in ("xb", "yb", "xc", "xcb", "r", "i", "s", "h", "A", "t1", "t2"))
                if j + 1 < NBLK:
                    lru_loads(j + 1)
                fw.op("dve", lambda v, j=j: v.tensor_copy(out=xb_t.t[:, 0:3], in_=xh.t[:, 3 * j:3 * j + 3]), [xb_t[:, :]], [xh[:, :]])
                fw.op("act", lambda a, j=j: a.activation(out=xc.t[:, :], in_=xb_t.t[:, 3:CH + 3], func=AF.Identity,
                                                         scale=par.t[:, j, 3:4], bias=par.t[:, j, 4:5]),
                      [xc[:, :]], [xb_t[:, :], par[:, :, :]])
                for s_ in (1, 2, 3):
                    fw.op("dve", lambda v, s_=s_, j=j: v.scalar_tensor_tensor(
                        out=xc.t[:, :], in0=xb_t.t[:, 3 - s_:3 - s_ + CH], scalar=par.t[:, j, 3 - s_:4 - s_], in1=xc.t[:, :],
                        op0=ALU.mult, op1=ALU.add), [xc[:, :]], [xb_t[:, :], xc[:, :], par[:, :, :]])
                fw.op("act", lambda a: a.activation(out=xcb.t[:, :], in_=xc.t[:, :], func=AF.Copy), [xcb[:, :]], [xc[:, :]])
                for q in range(4):
                    for (wb, dst, bcol) in ((wrb, r_t, 5), (wib, i_t, 6)):
                        bk = bank[bi_ % 8]
                        bi_ += 1
                        fw.op("pe", lambda pe, bk=bk, wb=wb, q=q, j=j: pe.matmul(
                            bk.t[:, :], lhsT=wb.t[:, j, :], rhs=xcb.t[:, q * 512:(q + 1) * 512], start=True, stop=True),
                            [bk[:, :]], [wb[:, :, :], xcb[:, :]])
                        fw.op("act", lambda a, bk=bk, dst=dst, q=q, j=j, bcol=bcol: a.activation(
                            out=dst.t[:, q * 512:(q + 1) * 512], in_=bk.t[:, :], func=AF.Sigmoid,
                            bias=par.t[:, j, bcol:bcol + 1]), [dst[:, :]], [bk[:, :], par[:, :, :]])
                fw.op("act", lambda a, j=j: a.activation(out=s_t.t[:, :], in_=r_t.t[:, :], func=AF.Exp, scale=cc_.t[:, j, 1:2]),
                      [s_t[:, :]], [r_t[:, :], cc_[:, :, :]])
                fw.op("act", lambda a, j=j: a.activation(out=r_t.t[:, :], in_=r_t.t[:, :], func=AF.Exp, scale=cc_.t[:, j, 0:1]),
                      [r_t[:, :]], [r_t[:, :], cc_[:, :, :]])
                fw.op("act", lambda a: a.activation(out=s_t.t[:, :], in_=s_t.t[:, :], func=AF.Sqrt, scale=-1.0, bias=onec.t[:, 0:1]),
                      [s_t[:, :]], [s_t[:, :], onec[:, :]])
                fw.op("act", lambda a: a.activation(out=t2.t[:, :], in_=yb_t.t[:, :], func=AF.Gelu_apprx_tanh), [t2[:, :]], [yb_t[:, :]])

            def stage_b(j):
                Z = SETS[j % 2]
                xb_t, yb_t, xc, xcb, r_t, i_t, s_t, h_t, A_t, t1, t2 = (Z[k] for k in ("xb", "yb", "xc", "xcb", "r", "i", "s", "h", "A", "t1", "t2"))
                fw.op("dve", lambda v: v.tensor_tensor(out=i_t.t[:, :], in0=i_t.t[:, :], in1=xc.t[:, :], op=ALU.mult),
                      [i_t[:, :]], [i_t[:, :], xc[:, :]])
                fw.op("dve", lambda v: v.tensor_tensor(out=i_t.t[:, :], in0=i_t.t[:, :], in1=s_t.t[:, :], op=ALU.mult),
                      [i_t[:, :]], [i_t[:, :], s_t[:, :]])
                fw.op("dve", lambda v: v.tensor_tensor_scan(out=h_t.t[:, :], data0=r_t.t[:, :], data1=i_t.t[:, :],
                                                            initial=0.0, op0=ALU.mult, op1=ALU.add),
                      [h_t[:, :]], [r_t[:, :], i_t[:, :]])
                fw.op("dve", lambda v: v.tensor_tensor_scan(out=A_t.t[:, :], data0=r_t.t[:, :], data1=zer.t[:, :],
                                                            initial=1.0, op0=ALU.mult, op1=ALU.add),
                      [A_t[:, :]], [r_t[:, :], zer[:, :]])
                fw.op("dve", lambda v, j=j: v.tensor_copy(out=st_sb.t[:, 2 * j:2 * j + 1], in_=A_t.t[:, CH - 1:CH]),
                      [st_sb[:, :]], [A_t[:, :]])
                fw.op("dve", lambda v, j=j: v.tensor_copy(out=st_sb.t[:, 2 * j + 1:2 * j + 2], in_=h_t.t[:, CH - 1:CH]),
                      [st_sb[:, :]], [h_t[:, :]])
                fw.op("dve", lambda v: v.tensor_tensor(out=h_t.t[:, :], in0=h_t.t[:, :], in1=t2.t[:, :], op=ALU.mult),
                      [h_t[:, :]], [t2[:, :], h_t[:, :]])
                fw.op("pool", lambda g: g.tensor_tensor(out=A_t.t[:, :], in0=A_t.t[:, :], in1=t2.t[:, :], op=ALU.mult),
                      [A_t[:, :]], [A_t[:, :], t2[:, :]])
                fw.dma("sp", P1s[j].v(P1s[j].t[j, :, :]), h_t[:, :])
                fw.dma("sp", P2s[j].v(P2s[j].t[j, :, :]), A_t[:, :])

            for j in range(NBLK + 1):
                if j < NBLK:
                    stage_a(j)
                if j >= 1:
                    stage_b(j - 1)
            fw.dma("sp", stx[:, :], st_sb[:, :])
            fw.barrier()
        fw.alloc = stack

        for i in range(4):
            fw.dma("sp", Vall.v(Vall.t[TOK + 512 * i:TOK + 512 * (i + 1), :]), vx[i][:, :])
            fw.dma("pool", Vall.v(Vall.t[512 * i:512 * (i + 1), :]), gV[i].v(gV[i].t[bass.ds(prev * 512, 512), :]))


        with contextlib.ExitStack() as sub:
            fw.alloc = sub
            qb = [[T(f"qb{g}_{h}", (128, TOK), BF16) for h in range(4)] for g in range(3)]
            kb = [T(f"kb{h}", (128, 2 * TOK), BF16) for h in range(4)]
            flag = T("flag_sb2", (128, 1))
            fw.dma("sp", flag[:, :], flagd[:, :])
            with contextlib.ExitStack() as sub2:
                fw.alloc = sub2
                stg_l = [T("stg0"), T("stg1")]; stgs_l = [T("stgs0", (32, CH)), T("stgs1", (32, CH))]
                stg_i = {"i": 0}
                posi = T("posi", (32, CH), I32); ki = T("ki", (32, CH), I32)
                ang = T("ang", (32, CH)); tr1 = T("tr1", (32, CH)); tr2 = T("tr2", (32, CH))
                Ct = T("Ct", (32, CH)); St = T("St", (32, CH)); cst = T("cst_sb", (32, 2))
                fw.dma("sp", cst[:, :], cstd[:, :])
                msk = T("msk_sb", (128, 4)); gs_sb = T("gs_sb", (128, 4, 24))
                hin = T("hin", (128, NBLK)); tA = T("tA", (128, NBLK)); tH = T("tH", (128, NBLK))
                fin = [(T("fin_a0"), T("fin_b0")), (T("fin_a1"), T("fin_b1"))]
                fw.dma("sp", msk[:, :], mskd[:, :])
                fw.allgather(gS, stx)
                fw.dma("sp", gs_sb[:, :, :], gS.v(gS.t[:, :].rearrange("(s p) c -> p s c", p=128)))
                fin_state = {"j": 0}

                def lru_chain():
                    fw.op("pool", lambda g: g.memset(hin.t[:, :], 0.0), [hin[:, :]], [])
                    for s_ in range(4):
                        pair = gs_sb.t[:, s_, :].rearrange("p (j two) -> p j two", two=2)
                        fw.op("dve", lambda v, s_=s_, pair=pair: v.tensor_scalar(out=tA.t[:, :], in0=pair[:, :, 0], scalar1=-1.0,
                                                                                scalar2=msk.t[:, s_:s_ + 1], op0=ALU.add, op1=ALU.mult),
                              [tA[:, :]], [gs_sb[:, :, :], msk[:, :]])
                        fw.op("dve", lambda v: v.tensor_scalar_add(out=tA.t[:, :], in0=tA.t[:, :], scalar1=1.0), [tA[:, :]], [tA[:, :]])
                        fw.op("dve", lambda v, s_=s_, pair=pair: v.tensor_scalar_mul(out=tH.t[:, :], in0=pair[:, :, 1],
                                                                                    scalar1=msk.t[:, s_:s_ + 1]),
                              [tH[:, :]], [gs_sb[:, :, :], msk[:, :]])
                        fw.op("dve", lambda v: v.tensor_tensor(out=hin.t[:, :], in0=hin.t[:, :], in1=tA.t[:, :], op=ALU.mult),
                              [hin[:, :]], [hin[:, :], tA[:, :]])
                        fw.op("dve", lambda v: v.tensor_tensor(out=hin.t[:, :], in0=hin.t[:, :], in1=tH.t[:, :], op=ALU.add),
                              [hin[:, :]], [hin[:, :], tH[:, :]])

                def fin_load(j):
                    p1, p2 = fin[j % 2]
                    fw.dma("pool", p1[:, :], P1s[j].v(P1s[j].t[j, :, :]))
                    fw.dma("pool", p2[:, :], P2s[j].v(P2s[j].t[j, :, :]))

                def fin_step():
                    j = fin_state["j"]
                    if j >= NBLK:
                        return
                    fin_state["j"] = j + 1
                    p1, p2 = fin[j % 2]
                    if j + 1 < NBLK:
                        fin_load(j + 1)
                    fw.op("dve", lambda v: v.scalar_tensor_tensor(
                        out=p2.t[:, :], in0=p2.t[:, :], scalar=hin.t[:, j:j + 1], in1=p1.t[:, :], op0=ALU.mult, op1=ALU.add),
                        [p2[:, :]], [p2[:, :], p1[:, :], hin[:, :]])
                    fw.dma("pool", mTs.v(mTs.t[512 + j * 128:512 + (j + 1) * 128, :]), p2[:, :])

                def reduce_sin(dst, shift):
                    A1 = [tr1[:, :]]; A2 = [tr2[:, :]]
                    fw.op("dve", lambda v: v.tensor_scalar(out=tr1.t[:, :], in0=ang.t[:, :], scalar1=float(shift),
                                                           scalar2=1.0 / (2 * PI), op0=ALU.add, op1=ALU.mult), A1, [ang[:, :]])
                    fw.op("dve", lambda v: v.tensor_copy(out=ki.t[:, :], in_=tr1.t[:, :]), [ki[:, :]], A1)
                    fw.op("dve", lambda v: v.tensor_copy(out=tr2.t[:, :], in_=ki.t[:, :]), A2, [ki[:, :]])
                    fw.op("dve", lambda v: v.tensor_scalar_add(out=tr1.t[:, :], in0=ang.t[:, :], scalar1=float(shift)), A1, [ang[:, :]])
                    for cval in (-C1, -C2):
                        fw.op("dve", lambda v, cval=cval: v.scalar_tensor_tensor(
                            out=tr1.t[:, :], in0=tr2.t[:, :], scalar=float(cval), in1=tr1.t[:, :], op0=ALU.mult, op1=ALU.add),
                            A1, A1 + A2)
                    fw.op("dve", lambda v: v.tensor_scalar(out=tr2.t[:, :], in0=tr1.t[:, :], scalar1=PI, scalar2=-2 * PI,
                                                           op0=ALU.is_gt, op1=ALU.mult), A2, A1)
                    fw.op("dve", lambda v: v.tensor_tensor(out=tr1.t[:, :], in0=tr1.t[:, :], in1=tr2.t[:, :], op=ALU.add), A1, A1 + A2)
                    fw.op("dve", lambda v: v.tensor_scalar(out=tr2.t[:, :], in0=tr1.t[:, :], scalar1=-PI, scalar2=2 * PI,
                                                           op0=ALU.is_lt, op1=ALU.mult), A2, A1)
                    fw.op("dve", lambda v: v.tensor_tensor(out=tr1.t[:, :], in0=tr1.t[:, :], in1=tr2.t[:, :], op=ALU.add), A1, A1 + A2)
                    fw.op("dve", lambda v: v.tensor_scalar(out=tr1.t[:, :], in0=tr1.t[:, :], scalar1=-PI_SAFE, scalar2=PI_SAFE,
                                                           op0=ALU.max, op1=ALU.min), A1, A1)
                    fw.op("act", lambda a: a.activation(out=dst.t[:, :], in_=tr1.t[:, :], func=AF.Sin), [dst[:, :]], A1)

                def next_stg():
                    stg_i["i"] += 1
                    return stg_l[stg_i["i"] % 2], stgs_l[stg_i["i"] % 2]

                def rotate(dstb, c0, stg, stgs):
                    fw.op("act", lambda a: a.activation(out=dstb.t[:, c0:c0 + CH], in_=stg.t[:, :], func=AF.Copy),
                          [dstb[:, :]], [stg[:, :]])
                    fw.op("dve", lambda v: v.tensor_tensor(out=tr1.t[:, :], in0=stg.t[0:32, :], in1=Ct.t[:, :], op=ALU.mult),
                          [tr1[:, :]], [stg[:, :], Ct[:, :]])
                    fw.op("dve", lambda v: v.tensor_tensor(out=tr2.t[:, :], in0=stgs.t[:, :], in1=St.t[:, :], op=ALU.mult),
                          [tr2[:, :]], [stgs[:, :], St[:, :]])
                    fw.op("dve", lambda v: v.tensor_tensor(out=dstb.t[0:32, c0:c0 + CH], in0=tr1.t[:, :], in1=tr2.t[:, :], op=ALU.add),
                          [dstb[:, :]], [tr1[:, :], tr2[:, :]])

                for half in (0, 1):
                    fw.dma("sp", posi[:, :], posd.v(posd.t[0:1, half * TOK:(half + 1) * TOK].partition_broadcast(32)))
                    fw.op("dve", lambda v: v.tensor_copy(out=ang.t[:, :], in_=posi.t[:, :]), [ang[:, :]], [posi[:, :]])
                    fw.op("dve", lambda v: v.tensor_scalar_mul(out=ang.t[:, :], in0=ang.t[:, :], scalar1=cst.t[:, 0:1]),
                          [ang[:, :]], [ang[:, :], cst[:, :]])
                    reduce_sin(St, 0.0)
                    fw.op("dve", lambda v: v.tensor_scalar_mul(out=St.t[:, :], in0=St.t[:, :], scalar1=cst.t[:, 1:2]),
                          [St[:, :]], [St[:, :], cst[:, :]])
                    reduce_sin(Ct, PI / 2)
                    if half == 1:
                        lru_chain()
                        fin_load(0)
                    for h in range(4):
                        if half == 0:
                            stg, stgs = next_stg()
                            fw.dma("pool", stg[:, :], gK[h].v(gK[h].t[bass.ds(prev * 128, 128), :]))
                            fw.dma("pool", V(stgs, stgs.t[0:16, :]), gK[h].v(gK[h].t[bass.ds(prev * 128 + 16, 16), :]))
                            fw.dma("pool", V(stgs, stgs.t[16:32, :]), gK[h].v(gK[h].t[bass.ds(prev * 128, 16), :]))
                            rotate(kb[h], 0, stg, stgs)
                        else:
                            stg, stgs = next_stg()
                            fw.dma("sp", stg[:, :], kx[h][:, :])
                            fw.dma("sp", V(stgs, stgs.t[0:16, :]), kx[h][16:32, :])
                            fw.dma("sp", V(stgs, stgs.t[16:32, :]), kx[h][0:16, :])
                            rotate(kb[h], TOK, stg, stgs)
                            fin_step()
                            for g in range(3):
                                r0 = g * 512 + h * 128
                                stg, stgs = next_stg()
                                fw.dma("sp", stg[:, :], projS[r0:r0 + 128, :])
                                fw.dma("sp", V(stgs, stgs.t[0:16, :]), projS[r0 + 16:r0 + 32, :])
                                fw.dma("sp", V(stgs, stgs.t[16:32, :]), projS[r0:r0 + 16, :])
                                rotate(qb[g][h], 0, stg, stgs)
                                fin_step()
                while fin_state["j"] < NBLK:
                    fin_step()
                fw.barrier()
            fw.alloc = sub
            Vt = T("Vt", (128, 32, 512), BF16)
            num = [T(f"num{h}", (128, TOK)) for h in range(4)]; den = [T(f"den{h}", (128, TOK)) for h in range(4)]
            ones_b = T("ones_b", (128, 128), BF16); onesf = T("onesf", (128, 512))
            NEG = -29952.0
            mc4 = T("mc4", (128, 512), BF16); mpA = T("mpA", (128, 512), BF16); mpB = T("mpB", (128, 512), BF16)
            mpC = T("mpC", (128, 512), BF16); idb = T("idb", (128, 128), BF16); mtmp = T("mtmp", (128, 512))
            pmc = [T(f"pmc{i}", (128, 512), BF16) for i in range(2)]; pmp = [T(f"pmp{i}", (128, 512), BF16) for i in range(2)]
            ostg = T("ostg", (128, TOK))
            fw.op("pool", lambda g: g.memset(ones_b.t[:, :], 1.0), [ones_b[:, :]], [])
            fw.op("pool", lambda g: g.memset(onesf.t[:, :], 0.0), [onesf[:, :]], [])
            for h in range(4):
                fw.op("pool", lambda g, h=h: g.memset(num[h].t[:, :], 0.0), [num[h][:, :]], [])
                fw.op("pool", lambda g, h=h: g.memset(den[h].t[:, :], 0.0), [den[h][:, :]], [])
            fw.op("pool", lambda g: g.affine_select(out=mc4.t[:, :], in_=onesf.t[:, :], pattern=[[0, 4], [1, 128]],
                                                    compare_op=ALU.is_ge, fill=NEG, base=0, channel_multiplier=-1),
                  [mc4[:, :]], [onesf[:, :]])
            fw.op("pool", lambda g: g.affine_select(out=mtmp.t[:, :], in_=onesf.t[:, :], pattern=[[0, 4], [-1, 128]],
                                                    compare_op=ALU.is_ge, fill=NEG, base=0, channel_multiplier=1),
                  [mtmp[:, :]], [onesf[:, :]])
            fw.op("dve", lambda v: v.tensor_copy(out=mpB.t[:, :], in_=mtmp.t[:, :]), [mpB[:, :]], [mtmp[:, :]])
            fw.op("dve", lambda v: v.tensor_scalar(out=mtmp.t[:, :], in0=mtmp.t[:, :], scalar1=-NEG, scalar2=flag.t[:, 0:1],
                                                   op0=ALU.add, op1=ALU.mult), [mtmp[:, :]], [mtmp[:, :], flag[:, :]])
            fw.op("dve", lambda v: v.tensor_scalar_add(out=mpC.t[:, :], in0=mtmp.t[:, :], scalar1=NEG), [mpC[:, :]], [mtmp[:, :]])
            fw.op("dve", lambda v: v.tensor_copy(out=mpA.t[:, :], in_=mpB.t[:, :]), [mpA[:, :]], [mpB[:, :]])
            fw.op("dve", lambda v: v.tensor_copy(out=mpA.t[:, 0:128], in_=mpC.t[:, 0:128]), [mpA[:, :]], [mpC[:, :]])
            fw.op("pool", lambda g: g.memset(mtmp.t[:, 0:128], 1.0), [mtmp[:, :]], [])
            fw.op("pool", lambda g: g.affine_select(out=idb.t[:, :], in_=mtmp.t[:, 0:128], pattern=[[-1, 128]],
                                                    compare_op=ALU.is_equal, fill=0.0, base=0, channel_multiplier=1),
                  [idb[:, :]], [mtmp[:, :]])
            grp = 0
            for g, (w_, d) in enumerate(PATTERNS):
                nb_all = 2 * TOK // d // 128
                half_n = nb_all // 2
                vperm = Vall.t[:, :].rearrange("(m r) c -> r m c", r=d)
                for r in range(d):
                    fw.dma("pool", Vt[:, r * nb_all:(r + 1) * nb_all, :],
                           Vall.v(vperm[r].rearrange("(n i) c -> i n c", i=128)), cont=(r > 0))
                blocks = [(r, n) for r in range(d) for n in range(half_n, nb_all)]
                for h in range(4):
                    qv = qb[g][h].t[:, :].rearrange("p (m r) -> p r m", r=d)
                    kv = kb[h].t[:, :].rearrange("p (m r) -> p r m", r=d)
                    numv = num[h].t[:, :].rearrange("p (m r) -> p r m", r=d)
                    denv = den[h].t[:, :].rearrange("p (m r) -> p r m", r=d)
                    for g0 in range(0, 16, 4):
                        gb = blocks[g0:g0 + 4]
                        par_ = grp % 2
                        grp += 1
                        S, P, O, Dn = (bank[4 * par_ + i] for i in range(4))
                        if d == 1:
                            mprev = mpA if g0 == 0 else mpB
                        elif d == 4:
                            mprev = mpA
                        else:
                            mprev = mpC
                        for bi, (r, n) in enumerate(gb):
                            nq = n - half_n
                            cs = slice(bi * 128, (bi + 1) * 128)
                            qs_ = qv[:, r, nq * 128:(nq + 1) * 128]
                            fw.op("pe", lambda pe, cs=cs, r=r, n=n, qs_=qs_, bi=bi: pe.matmul(
                                S.t[:, cs], lhsT=kv[:, r, n * 128:(n + 1) * 128], rhs=qs_, start=(bi == 0), stop=False),
                                [S[:, :]], [kb[h][:, :], qb[g][h][:, :]])
                            fw.op("pe", lambda pe, cs=cs, r=r, n=n, qs_=qs_, bi=bi: pe.matmul(
                                P.t[:, cs], lhsT=kv[:, r, (n - 1) * 128:n * 128], rhs=qs_, start=(bi == 0), stop=False),
                                [P[:, :]], [kb[h][:, :], qb[g][h][:, :]])
                        fw.op("pe", lambda pe: pe.matmul(S.t[:, :], lhsT=idb.t[:, :], rhs=mc4.t[:, :], start=False, stop=True),
                              [S[:, :]], [idb[:, :], mc4[:, :]])
                        fw.op("pe", lambda pe: pe.matmul(P.t[:, :], lhsT=idb.t[:, :], rhs=mprev.t[:, :], start=False, stop=True),
                              [P[:, :]], [idb[:, :], mprev[:, :]])
                        pmc_, pmp_ = pmc[par_], pmp[par_]
                        fw.op("act", lambda a: a.activation(out=pmc_.t[:, :], in_=S.t[:, :], func=AF.Exp, scale=SC), [pmc_[:, :]], [S[:, :]])
                        fw.op("act", lambda a: a.activation(out=pmp_.t[:, :], in_=P.t[:, :], func=AF.Exp, scale=SC), [pmp_[:, :]], [P[:, :]])
                        for bi, (r, n) in enumerate(gb):
                            B = r * nb_all + n
                            cs = slice(bi * 128, (bi + 1) * 128)
                            hs = slice(h * 128, (h + 1) * 128)
                            for (ob, use_v) in ((O, True), (Dn, False)):
                                l0 = Vt.t[:, B, hs] if use_v else ones_b.t[:, :]
                                l1 = Vt.t[:, B - 1, hs] if use_v else ones_b.t[:, :]
                                fw.op("pe", lambda pe, ob=ob, l0=l0, cs=cs: pe.matmul(ob.t[:, cs], lhsT=l0, rhs=pmc_.t[:, cs],
                                                                                      start=True, stop=False),
                                      [ob[:, :]], [Vt[:, :, :], ones_b[:, :], pmc_[:, :]])
                                fw.op("pe", lambda pe, ob=ob, l1=l1, cs=cs: pe.matmul(ob.t[:, cs], lhsT=l1, rhs=pmp_.t[:, cs],
                                                                                      start=False, stop=True),
                                      [ob[:, :]], [Vt[:, :, :], ones_b[:, :], pmp_[:, :]])
                        r0_, n0_ = gb[0]
                        nq0 = n0_ - half_n
                        if d == 16:
                            nsl = (numv[:, r0_:r0_ + 4, 0:128], denv[:, r0_:r0_ + 4, 0:128])
                            osl = (O.t[:, :].rearrange("p (a b) -> p a b", a=4), Dn.t[:, :].rearrange("p (a b) -> p a b", a=4))
                        else:
                            nsl = (numv[:, r0_, nq0 * 128:(nq0 + 4) * 128], denv[:, r0_, nq0 * 128:(nq0 + 4) * 128])
                            osl = (O.t[:, :], Dn.t[:, :])
                        fw.op("dve", lambda v: v.tensor_tensor(out=nsl[0], in0=osl[0], in1=nsl[0], op=ALU.add),
                              [num[h][:, :]], [O[:, :], num[h][:, :]])
                        fw.op("dve", lambda v: v.tensor_tensor(out=nsl[1], in0=osl[1], in1=nsl[1], op=ALU.add),
                              [den[h][:, :]], [Dn[:, :], den[h][:, :]])
            for h in range(4):
                fw.op("dve", lambda v, h=h: v.reciprocal(out=ostg.t[:, :], in_=den[h].t[:, :]), [ostg[:, :]], [den[h][:, :]])
                fw.op("dve", lambda v, h=h: v.tensor_tensor(out=ostg.t[:, :], in0=ostg.t[:, :], in1=num[h].t[:, :], op=ALU.mult),
                      [ostg[:, :]], [ostg[:, :], num[h][:, :]])
                fw.dma("sp", mTs.v(mTs.t[h * 128:(h + 1) * 128, :]), ostg[:, :])
            fw.barrier()
        fw.alloc = stack

        with contextlib.ExitStack() as sub:
            fw.alloc = sub
            dn = Dense(fw, nslot=3, banks=bank, pfx="c_")
            pTs = Buf("pTs", fw.sb("pTs", [128, 2, TP], BF16))
            for p in range(NPASS):
                tok0 = p * TP
                dn.load_x(x1s, tok0)
                dn.load_gb(g2, b2)
                msrc = mTs.t[:, tok0:tok0 + TP].rearrange("(k p) t -> p k t", p=128)
                for k0 in range(0, KD, 4):
                    fw.dma("pool", dn.xT[:, k0:k0 + 4, :], mTs.v(msrc[:, k0:k0 + 4, :]), cont=(k0 > 0))
                fw.dma("pool", pTs[:, :, :], pT.v(pT.t[:, tok0:tok0 + TP].rearrange("(k p) t -> p k t", p=128)))
                dn.scale_x(ALPHA)
                dn.proj_tm(wout, 0, KD, lambda kk, t: dn.xT.t[:, kk, t * 128:(t + 1) * 128], lambda kk: [dn.xT[:, :, :]], 1.0)
                dn.prefetch([(wg2, 0, KD, 0), (wu2, 0, KD, 0)])
                dn.layer_norm()
                dn.to_feature_major(affine=True)
                dn.layer_norm_finish()
                dn.load_gb(g3, b3)
                dn.ffn(wg2, wu2, wd2, prescale=ALPHA)
                dn.prefetch([(wpg, 0, KD, 0), (wpp, 0, 2, 0)])
                dn.layer_norm()
                dn.to_feature_major(affine=True)
                dn.layer_norm_finish()
                for g in range(D // 512):
                    sg, ag = dn.load_w(wpg, 0, KD, g * 512)
                    sp_, ap_ = dn.load_w(wpp, 0, 2, g * 512)
                    for t in range(dn.NT):
                        bg = dn.bank[(2 * t) % 8]
                        bp = dn.bank[(2 * t + 1) % 8]
                        ts_ = slice(t * 128, (t + 1) * 128)
                        for k in range(KD):
                            fw.op("pe", lambda pe, k=k, bg=bg, ts_=ts_: pe.matmul(
                                bg.t[:, :], lhsT=dn.xT.t[:, k, ts_], rhs=ag[:, k, :], start=(k == 0), stop=(k == KD - 1)),
                                [bg[:, :]], [sg[:, :, :], dn.xT[:, :, :]])
                        for k in range(2):
                            fw.op("pe", lambda pe, k=k, bp=bp, ts_=ts_: pe.matmul(
                                bp.t[:, :], lhsT=pTs.t[:, k, ts_], rhs=ap_[:, k, :], start=(k == 0), stop=(k == 1)),
                                [bp[:, :]], [sp_[:, :, :], pTs[:, :, :]])
                        gs = dn.gsb[t % 2]
                        fw.op("act", lambda a, gs=gs, bg=bg: a.activation(out=gs.t[:, :], in_=bg.t[:, :], func=AF.Sigmoid),
                              [gs[:, :]], [bg[:, :]])
                        fw.op("dve", lambda v, gs=gs, bp=bp: v.tensor_tensor(out=gs.t[:, :], in0=bp.t[:, :], in1=gs.t[:, :], op=ALU.mult),
                              [gs[:, :]], [bp[:, :], gs[:, :]])
                        xs = dn.xin[t].t[:, t, g * 512:(g + 1) * 512]
                        fw.op("dve", lambda v, gs=gs, xs=xs: v.tensor_tensor(out=xs, in0=gs.t[:, :], in1=xs, op=ALU.add),
                              [dn.xin[t][:, t, :]], [gs[:, :], dn.xin[t][:, t, :]])
                dn.store_x(out, tok0)
            fw.finish([out])
            fw.barrier()
        fw.alloc = stack
    return nc


def kernel(**inputs):
    inp = {k: np.asarray(v) for k, v in inputs.items()}
    ca = np.ascontiguousarray
    nc = build_fused()
    xs = inp["x"].reshape(BATCH * SEQ, D)
    pT = ca(inp["p"][0].reshape(BATCH * SEQ, 256).T)
    invf = np.power(np.float32(500000.0), -np.arange(16, dtype=np.float32) * np.float32(2.0 / 32)).astype(np.float32)
    cst = np.zeros((32, 2), np.float32)
    cst[:16, 0] = invf; cst[16:, 0] = invf; cst[:16, 1] = -1.0; cst[16:, 1] = 1.0
    par = np.zeros((NBLK, 128, 8), np.float32)
    for j in range(NBLK):
        sl = slice(j * 128, (j + 1) * 128)
        par[j, :, 0:4] = inp["conv_w"][0][:, sl].T
        par[j, :, 4] = inp["conv_b"][0][sl]; par[j, :, 5] = inp["b_rgate"][0][sl]
        par[j, :, 6] = inp["b_igate"][0][sl]; par[j, :, 7] = inp["lru_lambda"][0][sl]
    common = {
        "cst": cst, "par": par, "wr": ca(inp["w_rgate"][0]), "wi": ca(inp["w_igate"][0]),
        "wg1": ca(inp["ffn1_w_gate"][0]), "wu1": ca(inp["ffn1_w_up"][0]), "wd1": ca(inp["ffn1_w_down"][0]),
        "g1": ca(inp["ln1_g"]), "b1": ca(inp["ln1_b"]), "win": ca(inp["w_in"][0]),
        "wout": ca(inp["w_out"][0]), "g2": ca(inp["ln2_g"]), "b2": ca(inp["ln2_b"]),
        "wg2": ca(inp["ffn2_w_gate"][0]), "wu2": ca(inp["ffn2_w_up"][0]), "wd2": ca(inp["ffn2_w_down"][0]),
        "g3": ca(inp["ln3_g"]), "b3": ca(inp["ln3_b"]), "wpg": ca(inp["w_ple_gate"][0]), "wpp": ca(inp["w_ple_proj"][0]),
    }
    in_maps = []
    for c in range(NCORES):
        b, lr = divmod(c, 4)
        pos = inp["positions"][b].astype(np.int32)
        own = pos[lr * TOK:(lr + 1) * TOK]
        halo = pos[(lr - 1) * TOK:lr * TOK] if lr > 0 else own
        msk = np.zeros((128, 4), np.float32)
        msk[:, :lr] = 1.0
        m = dict(common, x=ca(xs[c * TOK:(c + 1) * TOK]), pT=ca(pT[:, c * TOK:(c + 1) * TOK]),
                 pos=ca(np.concatenate([halo, own])[None, :]), flag=np.full((128, 1), 1.0 if lr > 0 else 0.0, np.float32),
                 msk=msk)
        in_maps.append(m)
    res = run_bass_kernel_spmd(nc, in_maps, core_ids=list(range(NCORES)))
    out = np.concatenate([r["out"] for r in res.results], axis=0)
    return out.reshape(BATCH, SEQ, D).astype(np.float32)
```

```python
import contextlib
import numpy as np
import concourse.bass as bass
import concourse.mybir as mybir
from concourse.bass_utils import run_bass_kernel_spmd

F32 = mybir.dt.float32
BF16 = mybir.dt.bfloat16
I32 = mybir.dt.int32
AF = mybir.ActivationFunctionType
ALU = mybir.AluOpType
AX = mybir.AxisListType

NCORES = 8
D = 2048
DFF = 5632
NPROJ = 5632
SEQ = 8192
BATCH = 2
TOK = 2048
TP = 1024
NPASS = TOK // TP
ALPHA = 2.0 ** 0.25
EPS = 1e-5
KD = D // 128
KF = DFF // 128


class View:
    __slots__ = ("buf", "ap")

    def __init__(self, buf, ap):
        self.buf = buf
        self.ap = ap


class Buf:
    def __init__(self, name, t):
        self.name = name
        self.t = t
        self.w = {}
        self.r = {}
        self.dkey = None

    def __getitem__(self, idx):
        return View(self, self.t[idx])

    def v(self, ap):
        return View(self, ap)


class FW:
    def __init__(self, nc, stack):
        self.nc = nc
        self.stack = stack
        self.alloc = stack
        self.eng = {"pe": nc.tensor, "act": nc.scalar, "dve": nc.vector, "pool": nc.gpsimd, "sp": nc.sync}
        self.sems = {}
        self.cnt = {}
        self.known = {e: {} for e in self.eng}
        for e in ("pe", "act", "dve", "pool"):
            self._newsem(e)

    def _newsem(self, key):
        self.sems[key] = self.stack.enter_context(self.nc.semaphore("s_" + key))
        self.cnt[key] = 0

    def sb(self, name, shape, dt):
        return self.alloc.enter_context(self.nc.sbuf_tensor(name, list(shape), dt))

    def ps(self, name, shape, dt):
        return self.alloc.enter_context(self.nc.psum_tensor(name, list(shape), dt))

    def dram(self, name, shape, dt, kind):
        return self.nc.dram_tensor(name, list(shape), dt, kind=kind)

    def _wait(self, e, needs):
        for key, val in needs.items():
            if e == "pe" and key == "pe":
                continue
            if self.known[e].get(key, 0) < val:
                self.eng[e].wait_ge(self.sems[key], val)
                self.known[e][key] = val

    @staticmethod
    def _needs(outs, ins):
        needs = {}

        def need(d):
            for k, v in d.items():
                if needs.get(k, 0) < v:
                    needs[k] = v

        for v in ins:
            need(v.buf.w)
        for v in outs:
            need(v.buf.w)
            need(v.buf.r)
        return needs

    def op(self, e, fn, outs, ins):
        self._wait(e, self._needs(outs, ins))
        inst = fn(self.eng[e])
        self.cnt[e] += 1
        c = self.cnt[e]
        inst.then_inc(self.sems[e], 1)
        for v in ins:
            v.buf.r[e] = c
        for v in outs:
            v.buf.w[e] = c
            v.buf.r = {}
        return inst

    def dma(self, q, out, in_, cont=False, **kw):
        needs = self._needs([out], [in_])
        if cont and out.buf.dkey is not None:
            needs.pop(out.buf.dkey, None)
        self._wait(q, needs)
        inst = self.eng[q].dma_start(out=out.ap, in_=in_.ap, **kw)
        key = out.buf.dkey
        if key is None:
            key = "d_" + out.buf.name
            out.buf.dkey = key
            self._newsem(key)
        self.cnt[key] += 16
        c = self.cnt[key]
        inst.then_inc(self.sems[key], 16)
        in_.buf.r[key] = c
        out.buf.w[key] = c
        out.buf.r = {}
        return inst

    def allgather(self, out, in_):
        self._wait("pool", self._needs([out[:, :]], [in_[:, :]]))
        inst = self.eng["pool"].collective_compute(
            "AllGather", ALU.bypass, replica_groups=[[0, 1, 2, 3], [4, 5, 6, 7]],
            ins=[in_.t.ap().opt()], outs=[out.t.ap().opt()])
        key = "c_" + out.name
        self._newsem(key)
        self.cnt[key] += 1
        inst.then_inc(self.sems[key])
        in_.r[key] = 1
        out.w[key] = 1
        out.r = {}

    def barrier(self):
        for e in ("pe", "act", "dve", "pool", "sp"):
            self._wait(e, dict(self.cnt))

    def finish(self, bufs, e="sp"):
        needs = {}
        for b in bufs:
            for k, v in b.w.items():
                needs[k] = max(needs.get(k, 0), v)
        self._wait(e, needs)


def make_ident(fw, ident):
    fw.op("pool", lambda g: g.memset(ident[:, :].ap, 1.0), [ident[:, :]], [])
    fw.op("pool", lambda g: g.affine_select(out=ident[:, :].ap, in_=ident[:, :].ap, pattern=[[-1, 128]],
                                            compare_op=ALU.is_equal, fill=0.0, base=0, channel_multiplier=1),
          [ident[:, :]], [ident[:, :]])


class Dense:
    def __init__(self, fw, nslot, banks, pfx):
        self.fw = fw
        self.pfx = pfx
        self.NT = TP // 128
        self.NH = TP // 512
        self.ident = Buf(pfx + "ident", fw.sb(pfx + "ident", [128, 128], F32))
        make_ident(fw, self.ident)
        xin_t = fw.sb(pfx + "xin", [128, self.NT, D], F32)
        self.xin = [Buf(f"{pfx}xin{t}", xin_t) for t in range(self.NT)]
        self.xT = Buf(pfx + "xT", fw.sb(pfx + "xT", [128, KD, TP], BF16))
        self.HG = 12
        hT_t = fw.sb(pfx + "hT", [128, self.HG, TP], BF16)
        self.hT = [Buf(f"{pfx}hT{j}", hT_t) for j in range(self.HG)]
        self.NSLOT = nslot
        self.wt = fw.sb(pfx + "wbuf", [128, nslot, 8192], BF16)
        self.wslot = [Buf(f"{pfx}w{s}", self.wt) for s in range(nslot)]
        self.wi = 0
        self.pref = {}
        self.gsb = [Buf(f"{pfx}gsb{i}", fw.sb(f"{pfx}gsb{i}", [128, 512], F32)) for i in range(2)]
        stat_t = fw.sb(pfx + "stat", [128, self.NT, 24], F32)
        mv_t = fw.sb(pfx + "mv", [128, self.NT, 8], F32)
        self.stat = [Buf(f"{pfx}stat{t}", stat_t) for t in range(self.NT)]
        self.mv = [Buf(f"{pfx}mv{t}", mv_t) for t in range(self.NT)]
        self.gB = Buf(pfx + "gB", fw.sb(pfx + "gB", [128, D], F32))
        self.bB = Buf(pfx + "bB", fw.sb(pfx + "bB", [128, D], F32))
        self.bank = banks
        self.gi = 0
        self.ti = 0

    def prefetch(self, specs):
        for (wdram, row0, nk, col0) in specs:
            self.pref[(wdram.name, row0, nk, col0)] = self.load_w(wdram, row0, nk, col0)

    def load_w(self, wdram, row0, nk, col0, ncols=512):
        key = (wdram.name, row0, nk, col0)
        if key in self.pref:
            return self.pref.pop(key)
        s = self.wi % self.NSLOT
        self.wi += 1
        slot = self.wslot[s]
        src = wdram.t[row0:row0 + nk * 128, col0:col0 + ncols].rearrange("(k p) n -> p k n", p=128)
        dst = self.wt[:, s, 0:nk * ncols].rearrange("p (k n) -> p k n", n=ncols)
        step = 4
        for k0 in range(0, nk, step):
            k1 = min(nk, k0 + step)
            self.fw.dma("pool", slot.v(dst[:, k0:k1, :]), wdram.v(src[:, k0:k1, :]), cont=(k0 > 0))
        return slot, dst

    def load_gb(self, gvec, bvec):
        self.fw.dma("sp", self.gB[:, :], gvec.v(gvec.t[0:1, :].partition_broadcast(128)))
        self.fw.dma("sp", self.bB[:, :], bvec.v(bvec.t[0:1, :].partition_broadcast(128)))

    def to_feature_major(self):
        fw = self.fw
        for k in range(KD):
            for hf in range(self.NH):
                bk = self.bank[self.ti % 4]
                self.ti += 1
                for i in range(4):
                    t = hf * 4 + i
                    fw.op("pe", lambda pe, i=i, t=t, k=k, bk=bk: pe.transpose(
                        out=bk.t[:, i * 128:(i + 1) * 128], in_=self.xin[t].t[:, t, k * 128:(k + 1) * 128],
                        identity=self.ident.t[:, :]), [bk[:, :]], [self.xin[t][:, t, :], self.ident[:, :]])
                dst = self.xT.t[:, k, hf * 512:(hf + 1) * 512]
                if self.ti % 2 == 0:
                    fw.op("dve", lambda v, bk=bk, dst=dst: v.tensor_copy(out=dst, in_=bk.t[:, :]), [self.xT[:, :, :]], [bk[:, :]])
                else:
                    fw.op("act", lambda a, bk=bk, dst=dst: a.activation(out=dst, in_=bk.t[:, :], func=AF.Copy),
                          [self.xT[:, :, :]], [bk[:, :]])

    def gate_up(self, wg, wu, j0, j1):
        fw = self.fw
        for j4 in range(j0 // 4, j1 // 4):
            sg, ag = self.load_w(wg, 0, KD, j4 * 512)
            su, au = self.load_w(wu, 0, KD, j4 * 512)
            for jj in range(4):
                jl = j4 * 4 + jj - j0
                bset = 4 * (self.gi % 2)
                self.gi += 1
                for (sl, a, off) in ((sg, ag, 0), (su, au, 2)):
                    for hf in range(self.NH):
                        bk = self.bank[bset + off + hf]
                        for k in range(KD):
                            fw.op("pe", lambda pe, k=k, jj=jj, bk=bk, a=a, hf=hf: pe.matmul(
                                bk.t[:, :], lhsT=a[:, k, jj * 128:(jj + 1) * 128], rhs=self.xT.t[:, k, hf * 512:(hf + 1) * 512],
                                start=(k == 0), stop=(k == KD - 1)), [bk[:, :]], [sl[:, :, :], self.xT[:, :, :]])
                hb = self.hT[jl]
                for hf in range(self.NH):
                    bg = self.bank[bset + hf]
                    bu = self.bank[bset + 2 + hf]
                    gs = self.gsb[hf % 2]
                    fw.op("act", lambda a_, bg=bg, gs=gs: a_.activation(out=gs.t[:, :], in_=bg.t[:, :], func=AF.Silu),
                          [gs[:, :]], [bg[:, :]])
                    fw.op("dve", lambda v, bu=bu, gs=gs, jl=jl, hb=hb, hf=hf: v.tensor_tensor(
                        out=hb.t[:, jl, hf * 512:(hf + 1) * 512], in0=bu.t[:, :], in1=gs.t[:, :], op=ALU.mult),
                        [hb[:, jl, :]], [bu[:, :], gs[:, :]])

    def proj_tm(self, w, row0, nk, lhs_of, lhs_bufs, scale):
        fw = self.fw
        for g in range(D // 512):
            sl, a = self.load_w(w, row0, nk, g * 512)
            for t in range(self.NT):
                bk = self.bank[t % 8]
                for kk in range(nk):
                    fw.op("pe", lambda pe, kk=kk, t=t, bk=bk, a=a: pe.matmul(
                        bk.t[:, :], lhsT=lhs_of(kk, t), rhs=a[:, kk, :], start=(kk == 0), stop=(kk == nk - 1)),
                        [bk[:, :]], [sl[:, :, :]] + lhs_bufs(kk))
            for t in range(self.NT):
                bk = self.bank[t % 8]
                xs = self.xin[t].t[:, t, g * 512:(g + 1) * 512]
                fw.op("dve", lambda v, bk=bk, xs=xs: v.scalar_tensor_tensor(
                    out=xs, in0=bk.t[:, :], scalar=float(scale), in1=xs, op0=ALU.mult, op1=ALU.add),
                    [self.xin[t][:, t, :]], [bk[:, :], self.xin[t][:, t, :]])

    def ffn(self, wg, wu, wd):
        for j0 in range(0, KF, self.HG):
            j1 = min(KF, j0 + self.HG)
            self.gate_up(wg, wu, j0, j1)
            self.proj_tm(wd, j0 * 128, j1 - j0, lambda kk, t: self.hT[kk].t[:, kk, t * 128:(t + 1) * 128],
                         lambda kk: [self.hT[kk][:, kk, :]], 0.5)

    def layer_norm(self):
        fw = self.fw
        gB, bB = self.gB, self.bB

        def front(i):
            X = self.xin[i]
            xt = X.t[:, i, :]
            XV = [X[:, i, :]]
            st, mv = self.stat[i], self.mv[i]
            ST = [st[:, i, :]]
            MV = [mv[:, i, :]]
            m = lambda a, b: mv.t[:, i, a:b]
            for c in range(4):
                fw.op("dve", lambda v, c=c: v.bn_stats(out=st.t[:, i, c * 6:(c + 1) * 6], in_=X.t[:, i, c * 512:(c + 1) * 512]),
                      ST, XV)
            fw.op("dve", lambda v: v.bn_aggr(out=m(0, 2), in_=st.t[:, i, :]), MV, ST)
            fw.op("dve", lambda v: v.tensor_scalar_add(out=m(2, 3), in0=m(1, 2), scalar1=EPS), MV, MV)
            fw.op("act", lambda a: a.activation(out=m(3, 4), in_=m(2, 3), func=AF.Sqrt), MV, MV)
            fw.op("dve", lambda v: v.reciprocal(out=m(4, 5), in_=m(3, 4)), MV, MV)
            fw.op("dve", lambda v: v.scalar_tensor_tensor(out=m(5, 6), in0=m(0, 1), scalar=-1.0, in1=m(4, 5),
                                                          op0=ALU.mult, op1=ALU.mult), MV, MV)
            fw.op("act", lambda a: a.activation(out=xt, in_=xt, func=AF.Identity, bias=m(5, 6), scale=m(4, 5)), XV, XV + MV)

        def back(i):
            X = self.xin[i]
            xt = X.t[:, i, :]
            XV = [X[:, i, :]]
            fw.op("dve", lambda v: v.tensor_tensor(out=xt, in0=xt, in1=gB.t[:, :], op=ALU.mult), XV, XV + [gB[:, :]])
            fw.op("pool", lambda g: g.tensor_tensor(out=xt, in0=xt, in1=bB.t[:, :], op=ALU.add), XV, XV + [bB[:, :]])

        for i in range(self.NT + 1):
            if i < self.NT:
                front(i)
            if i >= 1:
                back(i - 1)

    def load_x(self, xd, tok0):
        for t in range(self.NT):
            self.fw.dma("sp", self.xin[t][:, t, :], xd.v(xd.t[tok0 + t * 128:tok0 + (t + 1) * 128, :]))

    def scale_x(self, s):
        for i in range(self.NT):
            self.fw.op("act", lambda a, i=i: a.mul(out=self.xin[i].t[:, i, :], in_=self.xin[i].t[:, i, :], mul=float(s)),
                       [self.xin[i][:, i, :]], [self.xin[i][:, i, :]])

    def store_x_tiles(self, ods, tok0):
        for t in range(self.NT):
            od = ods[t]
            self.fw.dma("sp", od.v(od.t[tok0 + t * 128:tok0 + (t + 1) * 128, :]), self.xin[t][:, t, :])

    def store_x(self, od, tok0):
        for t in range(self.NT):
            self.fw.dma("sp", od.v(od.t[tok0 + t * 128:tok0 + (t + 1) * 128, :]), self.xin[t][:, t, :], cont=(t > 0))
        for t in range(self.NT):
            self.xin[t].r[od.dkey] = self.fw.cnt[od.dkey]


PATTERNS = ((128, 1), (512, 4), (2048, 16))
CH = 2048
NCH = SEQ // CH
SC = 128.0 ** -0.5
PI = 3.141592653589793
C1 = 6.28125
C2 = 2 * PI - C1
PI_SAFE = 3.1415925


NBLK = 12


def build_fused():
    nc = bass.Bass("TRN2", target_bir_lowering=False)
    with contextlib.ExitStack() as stack:
        fw = FW(nc, stack)
        mk = lambda n, sh, kind="ExternalInput", dt=F32: Buf(n, fw.dram(n, sh, dt, kind))
        itn = lambda n, sh, dt=F32: Buf(n, fw.dram(n, sh, dt, "Internal"))
        x = mk("x", [TOK, D]); pT = mk("pT", [256, TOK]); posd = mk("pos", [1, 2 * TOK], dt=I32)
        flagd = mk("flag", [128, 1]); mskd = mk("msk", [128, 4]); cstd = mk("cst", [32, 2])
        wg1 = mk("wg1", [D, DFF]); wu1 = mk("wu1", [D, DFF]); wd1 = mk("wd1", [DFF, D])
        g1 = mk("g1", [1, D]); b1 = mk("b1", [1, D]); win = mk("win", [D, NPROJ])
        pard = mk("par", [NBLK, 128, 8]); wrd = mk("wr", [NBLK, 128, 128]); wid = mk("wi", [NBLK, 128, 128])
        wout = mk("wout", [D, D]); g2 = mk("g2", [1, D]); b2 = mk("b2", [1, D])
        wg2 = mk("wg2", [D, DFF]); wu2 = mk("wu2", [D, DFF]); wd2 = mk("wd2", [DFF, D])
        g3 = mk("g3", [1, D]); b3 = mk("b3", [1, D]); wpg = mk("wpg", [D, D]); wpp = mk("wpp", [256, D])
        out_t = fw.dram("out", [TOK, D], F32, "ExternalOutput")
        outs = [Buf(f"out{t}", out_t) for t in range(TP // 128)]
        x1s = itn("x1s", [TOK, D]); projS = itn("projS", [NPROJ, TOK]); mTs = itn("mTs", [D, TOK])
        kx = [itn(f"kx{h}", [128, TOK]) for h in range(4)]; vx = [itn(f"vx{i}", [512, 512]) for i in range(4)]
        gK = [itn(f"gK{h}", [512, TOK]) for h in range(4)]; gV = [itn(f"gV{i}", [2048, 512]) for i in range(4)]
        Vall = itn("Vall", [2 * TOK, 512])
        xbt = itn("xbt", [128, 36]); gX = itn("gX", [512, 36]); stx = itn("stx", [128, 24]); gS = itn("gS", [512, 24])
        P1t = fw.dram("P1s", [NBLK, 128, TOK], F32, "Internal"); P2t = fw.dram("P2s", [NBLK, 128, TOK], F32, "Internal")
        _p1 = [Buf(f"P1s{q}", P1t) for q in range(2)]; _p2 = [Buf(f"P2s{q}", P2t) for q in range(2)]
        P1s = [_p1[j % 2] for j in range(NBLK)]; P2s = [_p2[j % 2] for j in range(NBLK)]
        stack.enter_context(nc.Block())
        bank = [Buf(f"bank{i}", fw.ps(f"bank{i}", [128, 512], F32)) for i in range(8)]
        pid = nc.gpsimd.partition_id()
        prev = (pid + 3) % 4
        T = lambda n, sh=(128, CH), dt=F32: Buf(n, fw.sb(n, list(sh), dt))
        V = lambda b, ap: b.v(ap)

        with contextlib.ExitStack() as sub:
            fw.alloc = sub
            dn = Dense(fw, nslot=3, banks=bank, pfx="a_")
            pst = [Buf(f"pst{i}", fw.sb(f"pst{i}", [128, 512], F32)) for i in range(4)]
            pi = 0
            dn.load_gb(g1, b1)
            dn.load_x(x, 0)
            for p in range(NPASS):
                tok0 = p * TP
                dn.to_feature_major()
                dn.scale_x(ALPHA)
                dn.ffn(wg1, wu1, wd1)
                dn.prefetch([(win, 0, KD, 0), (win, 0, KD, 512)])
                dn.layer_norm()
                dn.store_x(x1s, tok0)
                dn.to_feature_major()
                if p + 1 < NPASS:
                    dn.load_x(x, tok0 + TP)
                for c4 in range(NPROJ // 512):
                    sl, a = dn.load_w(win, 0, KD, c4 * 512)
                    units = [(cc, hf) for cc in range(4) for hf in range(dn.NH)] if c4 != 4 else [(t, None) for t in range(dn.NT)]
                    for (u0, u1) in units:
                        bk = dn.bank[dn.ti % 8]
                        dn.ti += 1
                        st = pst[pi % 4]
                        pi += 1
                        if c4 == 4:
                            t = u0
                            for k in range(KD):
                                fw.op("pe", lambda pe, k=k, t=t, bk=bk, a=a: pe.matmul(
                                    bk.t[:, :], lhsT=dn.xT.t[:, k, t * 128:(t + 1) * 128], rhs=a[:, k, :],
                                    start=(k == 0), stop=(k == KD - 1)), [bk[:, :]], [sl[:, :, :], dn.xT[:, :, :]])
                        else:
                            cc, hf = u0, u1
                            for k in range(KD):
                                fw.op("pe", lambda pe, k=k, cc=cc, hf=hf, bk=bk, a=a: pe.matmul(
                                    bk.t[:, :], lhsT=a[:, k, cc * 128:(cc + 1) * 128], rhs=dn.xT.t[:, k, hf * 512:(hf + 1) * 512],
                                    start=(k == 0), stop=(k == KD - 1)), [bk[:, :]], [sl[:, :, :], dn.xT[:, :, :]])
                        if pi % 2 == 0:
                            fw.op("act", lambda a_, st=st, bk=bk: a_.activation(out=st.t[:, :], in_=bk.t[:, :], func=AF.Copy),
                                  [st[:, :]], [bk[:, :]])
                        else:
                            fw.op("dve", lambda v, st=st, bk=bk: v.tensor_copy(out=st.t[:, :], in_=bk.t[:, :]),
                                  [st[:, :]], [bk[:, :]])
                        if c4 == 4:
                            tg = tok0 + u0 * 128
                            vi, vr = tg // 512, tg % 512
                            fw.dma("sp", vx[vi].v(vx[vi].t[vr:vr + 128, :]), st[:, :])
                        else:
                            c = c4 * 4 + u0
                            cs_ = slice(tok0 + u1 * 512, tok0 + (u1 + 1) * 512)
                            if 12 <= c < 16:
                                fw.dma("sp", kx[c - 12].v(kx[c - 12].t[:, cs_]), st[:, :])
                            else:
                                fw.dma("sp", projS.v(projS.t[c * 128:(c + 1) * 128, cs_]), st[:, :])
            fw.barrier()
        fw.alloc = stack

        fw.dma("sp", xbt.v(xbt.t[:, :].rearrange("p (j s) -> p j s", s=3)),
               projS.v(projS.t[2560:2560 + 1536, TOK - 3:TOK].rearrange("(j p) s -> p j s", p=128)))
        fw.allgather(gX, xbt)
        for h in range(4):
            fw.allgather(gK[h], kx[h])
        for i in range(4):
            fw.allgather(gV[i], vx[i])


        with contextlib.ExitStack() as sub:
            fw.alloc = sub
            SETS = []
            for q_ in range(2):
                SETS.append(dict(xb=T(f"xb_t{q_}", (128, CH + 3)), yb=T(f"yb_t{q_}"), xc=T(f"xc{q_}"), xcb=T(f"xcb{q_}", dt=BF16),
                                 r=T(f"r_t{q_}"), i=T(f"i_t{q_}"), s=T(f"s_t{q_}"), h=T(f"h_t{q_}"), A=T(f"A_t{q_}"),
                                 t1=T(f"t1{q_}"), t2=T(f"t2{q_}")))
            zer = T("zer")
            par = T("par_sb", (128, NBLK, 8)); wrb = T("wrb", (128, NBLK, 128), BF16); wib = T("wib", (128, NBLK, 128), BF16)
            sc = T("sc", (128, 16)); cc_ = T("cc", (128, NBLK, 2)); st_sb = T("st_sb", (128, 24)); xh = T("xh", (128, 36))
            flag = T("flag_sb", (128, 1))
            fw.dma("sp", par[:, :, :], pard.v(pard.t[:, :, :].rearrange("j p c -> p j c")))
            fw.dma("pool", wrb[:, :, :], wrd.v(wrd.t[:, :, :].rearrange("j p c -> p j c")))
            fw.dma("pool", wib[:, :, :], wid.v(wid.t[:, :, :].rearrange("j p c -> p j c")))
            fw.dma("sp", flag[:, :], flagd[:, :])
            fw.dma("pool", xh[:, :], gX.v(gX.t[bass.ds(prev * 128, 128), :]))
            fw.op("dve", lambda v: v.tensor_scalar_mul(out=xh.t[:, :], in0=xh.t[:, :], scalar1=flag.t[:, 0:1]),
                  [xh[:, :]], [xh[:, :], flag[:, :]])
            fw.op("pool", lambda g: g.memset(zer.t[:, :], 0.0), [zer[:, :]], [])
            col = lambda i: sc.t[:, i:i + 1]
            S_ = [sc[:, :]]

            def dv(fn):
                fw.op("dve", fn, S_, S_ + [par[:, :, :]])

            for j in range(NBLK):
                lam = par.t[:, j, 7:8]
                dv(lambda v: v.tensor_scalar_mul(out=col(0), in0=lam, scalar1=-1.0))
                dv(lambda v: v.tensor_tensor(out=col(1), in0=col(0), in1=lam, op=ALU.max))
                fw.op("act", lambda a: a.activation(out=col(2), in_=col(1), func=AF.Exp, scale=-1.0), S_, S_)
                dv(lambda v: v.tensor_scalar_add(out=col(3), in0=col(2), scalar1=2.0))
                dv(lambda v: v.reciprocal(out=col(4), in_=col(3)))
                dv(lambda v: v.tensor_tensor(out=col(5), in0=col(2), in1=col(4), op=ALU.mult))
                dv(lambda v: v.tensor_tensor(out=col(6), in0=col(5), in1=col(5), op=ALU.mult))
                dv(lambda v: v.tensor_scalar(out=col(7), in0=col(6), scalar1=1.0 / 13, scalar2=1.0 / 11, op0=ALU.mult, op1=ALU.add))
                for cst in (1.0 / 9, 1.0 / 7, 1.0 / 5, 1.0 / 3, 1.0):
                    dv(lambda v, cst=cst: v.tensor_scalar(out=col(7), in0=col(7), scalar1=col(6), scalar2=cst,
                                                          op0=ALU.mult, op1=ALU.add))
                dv(lambda v: v.tensor_scalar(out=col(8), in0=col(7), scalar1=col(5), scalar2=2.0, op0=ALU.mult, op1=ALU.mult))
                dv(lambda v: v.tensor_scalar_max(out=col(9), in0=col(0), scalar1=0.0))
                dv(lambda v: v.tensor_tensor(out=col(10), in0=col(8), in1=col(9), op=ALU.add))
                fw.op("dve", lambda v, j=j: v.tensor_scalar_mul(out=cc_.t[:, j, 0:1], in0=col(10), scalar1=-8.0), [cc_[:, :, :]], S_)
                fw.op("dve", lambda v, j=j: v.tensor_scalar_mul(out=cc_.t[:, j, 1:2], in0=col(10), scalar1=-16.0), [cc_[:, :, :]], S_)

            bi_ = 0
            onec = T("onec", (128, 1))
            fw.op("pool", lambda g: g.memset(onec.t[:, :], 1.0), [onec[:, :]], [])

            def lru_loads(j):
                Z = SETS[j % 2]
                fw.dma("sp", V(Z["xb"], Z["xb"].t[:, 3:CH + 3]), projS.v(projS.t[2560 + j * 128:2560 + (j + 1) * 128, :]))
                fw.dma("sp", Z["yb"][:, :], projS.v(projS.t[4096 + j * 128:4096 + (j + 1) * 128, :]))

            lru_loads(0)

            def stage_a(j):
                nonlocal bi_
                Z = SETS[j % 2]
                xb_t, yb_t, xc, xcb, r_t, i_t, s_t, h_t, A_t, t1, t2 = (Z[k] for k in ("xb", "yb", "xc", "xcb", "r", "i", "s", "h", "A", "t1", "t2"))
                if j + 1 < NBLK:
                    lru_loads(j + 1)
                fw.op("dve", lambda v, j=j: v.tensor_copy(out=xb_t.t[:, 0:3], in_=xh.t[:, 3 * j:3 * j + 3]), [xb_t[:, :]], [xh[:, :]])
                fw.op("act", lambda a, j=j: a.activation(out=xc.t[:, :], in_=xb_t.t[:, 3:CH + 3], func=AF.Identity,
                                                         scale=par.t[:, j, 3:4], bias=par.t[:, j, 4:5]),
                      [xc[:, :]], [xb_t[:, :], par[:, :, :]])
                for s_ in (1, 2, 3):
                    fw.op("dve", lambda v, s_=s_, j=j: v.scalar_tensor_tensor(
                        out=xc.t[:, :], in0=xb_t.t[:, 3 - s_:3 - s_ + CH], scalar=par.t[:, j, 3 - s_:4 - s_], in1=xc.t[:, :],
                        op0=ALU.mult, op1=ALU.add), [xc[:, :]], [xb_t[:, :], xc[:, :], par[:, :, :]])
                fw.op("act", lambda a: a.activation(out=xcb.t[:, :], in_=xc.t[:, :], func=AF.Copy), [xcb[:, :]], [xc[:, :]])
                for q in range(4):
                    for (wb, dst, bcol) in ((wrb, r_t, 5), (wib, i_t, 6)):
                        bk = bank[bi_ % 8]
                        bi_ += 1
                        fw.op("pe", lambda pe, bk=bk, wb=wb, q=q, j=j: pe.matmul(
                            bk.t[:, :], lhsT=wb.t[:, j, :], rhs=xcb.t[:, q * 512:(q + 1) * 512], start=True, stop=True),
                            [bk[:, :]], [wb[:, :, :], xcb[:, :]])
                        fw.op("act", lambda a, bk=bk, dst=dst, q=q, j=j, bcol=bcol: a.activation(
                            out=dst.t[:, q * 512:(q + 1) * 512], in_=bk.t[:, :], func=AF.Sigmoid,
                            bias=par.t[:, j, bcol:bcol + 1]), [dst[:, :]], [bk[:, :], par[:, :, :]])
                fw.op("act", lambda a, j=j: a.activation(out=s_t.t[:, :], in_=r_t.t[:, :], func=AF.Exp, scale=cc_.t[:, j, 1:2]),
                      [s_t[:, :]], [r_t[:, :], cc_[:, :, :]])
                fw.op("act", lambda a, j=j: a.activation(out=r_t.t[:, :], in_=r_t.t[:, :], func=AF.Exp, scale=cc_.t[:, j, 0:1]),
                      [r_t[:, :]], [r_t[:, :], cc_[:, :, :]])
                fw.op("act", lambda a: a.activation(out=s_t.t[:, :], in_=s_t.t[:, :], func=AF.Sqrt, scale=-1.0, bias=onec.t[:, 0:1]),
                      [s_t[:, :]], [s_t[:, :], onec[:, :]])
                fw.op("act", lambda a: a.activation(out=t2.t[:, :], in_=yb_t.t[:, :], func=AF.Gelu_apprx_tanh), [t2[:, :]], [yb_t[:, :]])

            def stage_b(j):
                Z = SETS[j % 2]
                xb_t, yb_t, xc, xcb, r_t, i_t, s_t, h_t, A_t, t1, t2 = (Z[k] for k in ("xb", "yb", "xc", "xcb", "r", "i", "s", "h", "A", "t1", "t2"))
                fw.op("dve", lambda v: v.tensor_tensor(out=i_t.t[:, :], in0=i_t.t[:, :], in1=xc.t[:, :], op=ALU.mult),
                      [i_t[:, :]], [i_t[:, :], xc[:, :]])
                fw.op("dve", lambda v: v.tensor_tensor(out=i_t.t[:, :], in0=i_t.t[:, :], in1=s_t.t[:, :], op=ALU.mult),
                      [i_t[:, :]], [i_t[:, :], s_t[:, :]])
                fw.op("dve", lambda v: v.tensor_tensor_scan(out=h_t.t[:, :], data0=r_t.t[:, :], data1=i_t.t[:, :],
                                                            initial=0.0, op0=ALU.mult, op1=ALU.add),
                      [h_t[:, :]], [r_t[:, :], i_t[:, :]])
                fw.op("dve", lambda v: v.tensor_tensor_scan(out=A_t.t[:, :], data0=r_t.t[:, :], data1=zer.t[:, :],
                                                            initial=1.0, op0=ALU.mult, op1=ALU.add),
                      [A_t[:, :]], [r_t[:, :], zer[:, :]])
                fw.op("dve", lambda v, j=j: v.tensor_copy(out=st_sb.t[:, 2 * j:2 * j + 1], in_=A_t.t[:, CH - 1:CH]),
                      [st_sb[:, :]], [A_t[:, :]])
                fw.op("dve", lambda v, j=j: v.tensor_copy(out=st_sb.t[:, 2 * j + 1:2 * j + 2], in_=h_t.t[:, CH - 1:CH]),
                      [st_sb[:, :]], [h_t[:, :]])
                fw.op("dve", lambda v: v.tensor_tensor(out=h_t.t[:, :], in0=h_t.t[:, :], in1=t2.t[:, :], op=ALU.mult),
                      [h_t[:, :]], [t2[:, :], h_t[:, :]])
                fw.op("pool", lambda g: g.tensor_tensor(out=A_t.t[:, :], in0=A_t.t[:, :], in1=t2.t[:, :], op=ALU.mult),
                      [A_t[:, :]], [A_t[:, :], t2[:, :]])
                fw.dma("sp", P1s[j].v(P1s[j].t[j, :, :]), h_t[:, :])
                fw.dma("sp", P2s[j].v(P2s[j].t[j, :, :]), A_t[:, :])

            for j in range(NBLK + 1):
                if j < NBLK:
                    stage_a(j)
                if j >= 1:
                    stage_b(j - 1)
            fw.dma("sp", stx[:, :], st_sb[:, :])
            fw.barrier()
        fw.alloc = stack

        for i in range(4):
            fw.dma("sp", Vall.v(Vall.t[TOK + 512 * i:TOK + 512 * (i + 1), :]), vx[i][:, :])
            fw.dma("pool", Vall.v(Vall.t[512 * i:512 * (i + 1), :]), gV[i].v(gV[i].t[bass.ds(prev * 512, 512), :]))


        with contextlib.ExitStack() as sub:
            fw.alloc = sub
            qb = [[T(f"qb{g}_{h}", (128, TOK), BF16) for h in range(4)] for g in range(3)]
            kb = [T(f"kb{h}", (128, 2 * TOK), BF16) for h in range(4)]
            flag = T("flag_sb2", (128, 1))
            fw.dma("sp", flag[:, :], flagd[:, :])
            with contextlib.ExitStack() as sub2:
                fw.alloc = sub2
                stg_l = [T("stg0"), T("stg1")]; stgs_l = [T("stgs0", (32, CH)), T("stgs1", (32, CH))]
                stg_i = {"i": 0}
                posi = T("posi", (32, CH), I32); ki = T("ki", (32, CH), I32)
                ang = T("ang", (32, CH)); tr1 = T("tr1", (32, CH)); tr2 = T("tr2", (32, CH))
                Ct = T("Ct", (32, CH)); St = T("St", (32, CH)); cst = T("cst_sb", (32, 2))
                fw.dma("sp", cst[:, :], cstd[:, :])
                msk = T("msk_sb", (128, 4)); gs_sb = T("gs_sb", (128, 4, 24))
                hin = T("hin", (128, NBLK)); tA = T("tA", (128, NBLK)); tH = T("tH", (128, NBLK))
                fin = [(T("fin_a0"), T("fin_b0")), (T("fin_a1"), T("fin_b1"))]
                fw.dma("sp", msk[:, :], mskd[:, :])
                fw.allgather(gS, stx)
                fw.dma("sp", gs_sb[:, :, :], gS.v(gS.t[:, :].rearrange("(s p) c -> p s c", p=128)))
                fin_state = {"j": 0}

                def lru_chain():
                    fw.op("pool", lambda g: g.memset(hin.t[:, :], 0.0), [hin[:, :]], [])
                    for s_ in range(4):
                        pair = gs_sb.t[:, s_, :].rearrange("p (j two) -> p j two", two=2)
                        fw.op("dve", lambda v, s_=s_, pair=pair: v.tensor_scalar(out=tA.t[:, :], in0=pair[:, :, 0], scalar1=-1.0,
                                                                                scalar2=msk.t[:, s_:s_ + 1], op0=ALU.add, op1=ALU.mult),
                              [tA[:, :]], [gs_sb[:, :, :], msk[:, :]])
                        fw.op("dve", lambda v: v.tensor_scalar_add(out=tA.t[:, :], in0=tA.t[:, :], scalar1=1.0), [tA[:, :]], [tA[:, :]])
                        fw.op("dve", lambda v, s_=s_, pair=pair: v.tensor_scalar_mul(out=tH.t[:, :], in0=pair[:, :, 1],
                                                                                    scalar1=msk.t[:, s_:s_ + 1]),
                              [tH[:, :]], [gs_sb[:, :, :], msk[:, :]])
                        fw.op("dve", lambda v: v.tensor_tensor(out=hin.t[:, :], in0=hin.t[:, :], in1=tA.t[:, :], op=ALU.mult),
                              [hin[:, :]], [hin[:, :], tA[:, :]])
                        fw.op("dve", lambda v: v.tensor_tensor(out=hin.t[:, :], in0=hin.t[:, :], in1=tH.t[:, :], op=ALU.add),
                              [hin[:, :]], [hin[:, :], tH[:, :]])

                def fin_load(j):
                    p1, p2 = fin[j % 2]
                    fw.dma("pool", p1[:, :], P1s[j].v(P1s[j].t[j, :, :]))
                    fw.dma("pool", p2[:, :], P2s[j].v(P2s[j].t[j, :, :]))

                def fin_step():
                    j = fin_state["j"]
                    if j >= NBLK:
                        return
                    fin_state["j"] = j + 1
                    p1, p2 = fin[j % 2]
                    if j + 1 < NBLK:
                        fin_load(j + 1)
                    fw.op("dve", lambda v: v.scalar_tensor_tensor(
                        out=p2.t[:, :], in0=p2.t[:, :], scalar=hin.t[:, j:j + 1], in1=p1.t[:, :], op0=ALU.mult, op1=ALU.add),
                        [p2[:, :]], [p2[:, :], p1[:, :], hin[:, :]])
                    fw.dma("pool", mTs.v(mTs.t[512 + j * 128:512 + (j + 1) * 128, :]), p2[:, :])

                def reduce_sin(dst, shift):
                    A1 = [tr1[:, :]]; A2 = [tr2[:, :]]
                    fw.op("dve", lambda v: v.tensor_scalar(out=tr1.t[:, :], in0=ang.t[:, :], scalar1=float(shift),
                                                           scalar2=1.0 / (2 * PI), op0=ALU.add, op1=ALU.mult), A1, [ang[:, :]])
                    fw.op("dve", lambda v: v.tensor_copy(out=ki.t[:, :], in_=tr1.t[:, :]), [ki[:, :]], A1)
                    fw.op("dve", lambda v: v.tensor_copy(out=tr2.t[:, :], in_=ki.t[:, :]), A2, [ki[:, :]])
                    fw.op("dve", lambda v: v.tensor_scalar_add(out=tr1.t[:, :], in0=ang.t[:, :], scalar1=float(shift)), A1, [ang[:, :]])
                    for cval in (-C1, -C2):
                        fw.op("dve", lambda v, cval=cval: v.scalar_tensor_tensor(
                            out=tr1.t[:, :], in0=tr2.t[:, :], scalar=float(cval), in1=tr1.t[:, :], op0=ALU.mult, op1=ALU.add),
                            A1, A1 + A2)
                    fw.op("dve", lambda v: v.tensor_scalar(out=tr2.t[:, :], in0=tr1.t[:, :], scalar1=PI, scalar2=-2 * PI,
                                                           op0=ALU.is_gt, op1=ALU.mult), A2, A1)
                    fw.op("dve", lambda v: v.tensor_tensor(out=tr1.t[:, :], in0=tr1.t[:, :], in1=tr2.t[:, :], op=ALU.add), A1, A1 + A2)
                    fw.op("dve", lambda v: v.tensor_scalar(out=tr2.t[:, :], in0=tr1.t[:, :], scalar1=-PI, scalar2=2 * PI,
                                                           op0=ALU.is_lt, op1=ALU.mult), A2, A1)
                    fw.op("dve", lambda v: v.tensor_tensor(out=tr1.t[:, :], in0=tr1.t[:, :], in1=tr2.t[:, :], op=ALU.add), A1, A1 + A2)
                    fw.op("dve", lambda v: v.tensor_scalar(out=tr1.t[:, :], in0=tr1.t[:, :], scalar1=-PI_SAFE, scalar2=PI_SAFE,
                                                           op0=ALU.max, op1=ALU.min), A1, A1)
                    fw.op("act", lambda a: a.activation(out=dst.t[:, :], in_=tr1.t[:, :], func=AF.Sin), [dst[:, :]], A1)

                def next_stg():
                    stg_i["i"] += 1
                    return stg_l[stg_i["i"] % 2], stgs_l[stg_i["i"] % 2]

                def rotate(dstb, c0, stg, stgs):
                    fw.op("act", lambda a: a.activation(out=dstb.t[:, c0:c0 + CH], in_=stg.t[:, :], func=AF.Copy),
                          [dstb[:, :]], [stg[:, :]])
                    fw.op("dve", lambda v: v.tensor_tensor(out=tr1.t[:, :], in0=stg.t[0:32, :], in1=Ct.t[:, :], op=ALU.mult),
                          [tr1[:, :]], [stg[:, :], Ct[:, :]])
                    fw.op("dve", lambda v: v.tensor_tensor(out=tr2.t[:, :], in0=stgs.t[:, :], in1=St.t[:, :], op=ALU.mult),
                          [tr2[:, :]], [stgs[:, :], St[:, :]])
                    fw.op("dve", lambda v: v.tensor_tensor(out=dstb.t[0:32, c0:c0 + CH], in0=tr1.t[:, :], in1=tr2.t[:, :], op=ALU.add),
                          [dstb[:, :]], [tr1[:, :], tr2[:, :]])

                for half in (0, 1):
                    fw.dma("sp", posi[:, :], posd.v(posd.t[0:1, half * TOK:(half + 1) * TOK].partition_broadcast(32)))
                    fw.op("dve", lambda v: v.tensor_copy(out=ang.t[:, :], in_=posi.t[:, :]), [ang[:, :]], [posi[:, :]])
                    fw.op("dve", lambda v: v.tensor_scalar_mul(out=ang.t[:, :], in0=ang.t[:, :], scalar1=cst.t[:, 0:1]),
                          [ang[:, :]], [ang[:, :], cst[:, :]])
                    reduce_sin(St, 0.0)
                    fw.op("dve", lambda v: v.tensor_scalar_mul(out=St.t[:, :], in0=St.t[:, :], scalar1=cst.t[:, 1:2]),
                          [St[:, :]], [St[:, :], cst[:, :]])
                    reduce_sin(Ct, PI / 2)
                    if half == 1:
                        lru_chain()
                        fin_load(0)
                    for h in range(4):
                        if half == 0:
                            stg, stgs = next_stg()
                            fw.dma("pool", stg[:, :], gK[h].v(gK[h].t[bass.ds(prev * 128, 128), :]))
                            fw.dma("pool", V(stgs, stgs.t[0:16, :]), gK[h].v(gK[h].t[bass.ds(prev * 128 + 16, 16), :]))
                            fw.dma("pool", V(stgs, stgs.t[16:32, :]), gK[h].v(gK[h].t[bass.ds(prev * 128, 16), :]))
                            rotate(kb[h], 0, stg, stgs)
                        else:
                            stg, stgs = next_stg()
                            fw.dma("sp", stg[:, :], kx[h][:, :])
                            fw.dma("sp", V(stgs, stgs.t[0:16, :]), kx[h][16:32, :])
                            fw.dma("sp", V(stgs, stgs.t[16:32, :]), kx[h][0:16, :])
                            rotate(kb[h], TOK, stg, stgs)
                            fin_step()
                            for g in range(3):
                                r0 = g * 512 + h * 128
                                stg, stgs = next_stg()
                                fw.dma("sp", stg[:, :], projS[r0:r0 + 128, :])
                                fw.dma("sp", V(stgs, stgs.t[0:16, :]), projS[r0 + 16:r0 + 32, :])
                                fw.dma("sp", V(stgs, stgs.t[16:32, :]), projS[r0:r0 + 16, :])
                                rotate(qb[g][h], 0, stg, stgs)
                                fin_step()
                while fin_state["j"] < NBLK:
                    fin_step()
                fw.barrier()
            fw.alloc = sub
            Vt = T("Vt", (128, 32, 512), BF16)
            num = [T(f"num{h}", (128, TOK)) for h in range(4)]; den = [T(f"den{h}", (128, TOK)) for h in range(4)]
            ones_b = T("ones_b", (128, 128), BF16); onesf = T("onesf", (128, 512))
            NEG = -29952.0
            mc4 = T("mc4", (128, 512), BF16); mpA = T("mpA", (128, 512), BF16); mpB = T("mpB", (128, 512), BF16)
            mpC = T("mpC", (128, 512), BF16); idb = T("idb", (128, 128), BF16); mtmp = T("mtmp", (128, 512))
            pmc = [T(f"pmc{i}", (128, 512), BF16) for i in range(2)]; pmp = [T(f"pmp{i}", (128, 512), BF16) for i in range(2)]
            ostg = T("ostg", (128, TOK))
            fw.op("pool", lambda g: g.memset(ones_b.t[:, :], 1.0), [ones_b[:, :]], [])
            fw.op("pool", lambda g: g.memset(onesf.t[:, :], 0.0), [onesf[:, :]], [])
            for h in range(4):
                fw.op("pool", lambda g, h=h: g.memset(num[h].t[:, :], 0.0), [num[h][:, :]], [])
                fw.op("pool", lambda g, h=h: g.memset(den[h].t[:, :], 0.0), [den[h][:, :]], [])
            fw.op("pool", lambda g: g.affine_select(out=mc4.t[:, :], in_=onesf.t[:, :], pattern=[[0, 4], [1, 128]],
                                                    compare_op=ALU.is_ge, fill=NEG, base=0, channel_multiplier=-1),
                  [mc4[:, :]], [onesf[:, :]])
            fw.op("pool", lambda g: g.affine_select(out=mtmp.t[:, :], in_=onesf.t[:, :], pattern=[[0, 4], [-1, 128]],
                                                    compare_op=ALU.is_ge, fill=NEG, base=0, channel_multiplier=1),
                  [mtmp[:, :]], [onesf[:, :]])
            fw.op("dve", lambda v: v.tensor_copy(out=mpB.t[:, :], in_=mtmp.t[:, :]), [mpB[:, :]], [mtmp[:, :]])
            fw.op("dve", lambda v: v.tensor_scalar(out=mtmp.t[:, :], in0=mtmp.t[:, :], scalar1=-NEG, scalar2=flag.t[:, 0:1],
                                                   op0=ALU.add, op1=ALU.mult), [mtmp[:, :]], [mtmp[:, :], flag[:, :]])
            fw.op("dve", lambda v: v.tensor_scalar_add(out=mpC.t[:, :], in0=mtmp.t[:, :], scalar1=NEG), [mpC[:, :]], [mtmp[:, :]])
            fw.op("dve", lambda v: v.tensor_copy(out=mpA.t[:, :], in_=mpB.t[:, :]), [mpA[:, :]], [mpB[:, :]])
            fw.op("dve", lambda v: v.tensor_copy(out=mpA.t[:, 0:128], in_=mpC.t[:, 0:128]), [mpA[:, :]], [mpC[:, :]])
            fw.op("pool", lambda g: g.memset(mtmp.t[:, 0:128], 1.0), [mtmp[:, :]], [])
            fw.op("pool", lambda g: g.affine_select(out=idb.t[:, :], in_=mtmp.t[:, 0:128], pattern=[[-1, 128]],
                                                    compare_op=ALU.is_equal, fill=0.0, base=0, channel_multiplier=1),
                  [idb[:, :]], [mtmp[:, :]])
            grp = 0
            for g, (w_, d) in enumerate(PATTERNS):
                nb_all = 2 * TOK // d // 128
                half_n = nb_all // 2
                vperm = Vall.t[:, :].rearrange("(m r) c -> r m c", r=d)
                for r in range(d):
                    fw.dma("pool", Vt[:, r * nb_all:(r + 1) * nb_all, :],
                           Vall.v(vperm[r].rearrange("(n i) c -> i n c", i=128)), cont=(r > 0))
                blocks = [(r, n) for r in range(d) for n in range(half_n, nb_all)]
                for h in range(4):
                    qv = qb[g][h].t[:, :].rearrange("p (m r) -> p r m", r=d)
                    kv = kb[h].t[:, :].rearrange("p (m r) -> p r m", r=d)
                    numv = num[h].t[:, :].rearrange("p (m r) -> p r m", r=d)
                    denv = den[h].t[:, :].rearrange("p (m r) -> p r m", r=d)
                    for g0 in range(0, 16, 4):
                        gb = blocks[g0:g0 + 4]
                        par_ = grp % 2
                        grp += 1
                        S, P, O, Dn = (bank[4 * par_ + i] for i in range(4))
                        if d == 1:
                            mprev = mpA if g0 == 0 else mpB
                        elif d == 4:
                            mprev = mpA
                        else:
                            mprev = mpC
                        for bi, (r, n) in enumerate(gb):
                            nq = n - half_n
                            cs = slice(bi * 128, (bi + 1) * 128)
                            qs_ = qv[:, r, nq * 128:(nq + 1) * 128]
                            fw.op("pe", lambda pe, cs=cs, r=r, n=n, qs_=qs_, bi=bi: pe.matmul(
                                S.t[:, cs], lhsT=kv[:, r, n * 128:(n + 1) * 128], rhs=qs_, start=(bi == 0), stop=False),
                                [S[:, :]], [kb[h][:, :], qb[g][h][:, :]])
                            fw.op("pe", lambda pe, cs=cs, r=r, n=n, qs_=qs_, bi=bi: pe.matmul(
                                P.t[:, cs], lhsT=kv[:, r, (n - 1) * 128:n * 128], rhs=qs_, start=(bi == 0), stop=False),
                                [P[:, :]], [kb[h][:, :], qb[g][h][:, :]])
                        fw.op("pe", lambda pe: pe.matmul(S.t[:, :], lhsT=idb.t[:, :], rhs=mc4.t[:, :], start=False, stop=True),
                              [S[:, :]], [idb[:, :], mc4[:, :]])
                        fw.op("pe", lambda pe: pe.matmul(P.t[:, :], lhsT=idb.t[:, :], rhs=mprev.t[:, :], start=False, stop=True),
                              [P[:, :]], [idb[:, :], mprev[:, :]])
                        pmc_, pmp_ = pmc[par_], pmp[par_]
                        fw.op("act", lambda a: a.activation(out=pmc_.t[:, :], in_=S.t[:, :], func=AF.Exp, scale=SC), [pmc_[:, :]], [S[:, :]])
                        fw.op("act", lambda a: a.activation(out=pmp_.t[:, :], in_=P.t[:, :], func=AF.Exp, scale=SC), [pmp_[:, :]], [P[:, :]])
                        for bi, (r, n) in enumerate(gb):
                            B = r * nb_all + n
                            cs = slice(bi * 128, (bi + 1) * 128)
                            hs = slice(h * 128, (h + 1) * 128)
                            for (ob, use_v) in ((O, True), (Dn, False)):
                                l0 = Vt.t[:, B, hs] if use_v else ones_b.t[:, :]
                                l1 = Vt.t[:, B - 1, hs] if use_v else ones_b.t[:, :]
                                fw.op("pe", lambda pe, ob=ob, l0=l0, cs=cs: pe.matmul(ob.t[:, cs], lhsT=l0, rhs=pmc_.t[:, cs],
                                                                                      start=True, stop=False),
                                      [ob[:, :]], [Vt[:, :, :], ones_b[:, :], pmc_[:, :]])
                                fw.op("pe", lambda pe, ob=ob, l1=l1, cs=cs: pe.matmul(ob.t[:, cs], lhsT=l1, rhs=pmp_.t[:, cs],
                                                                                      start=False, stop=True),
                                      [ob[:, :]], [Vt[:, :, :], ones_b[:, :], pmp_[:, :]])
                        r0_, n0_ = gb[0]
                        nq0 = n0_ - half_n
                        if d == 16:
                            nsl = (numv[:, r0_:r0_ + 4, 0:128], denv[:, r0_:r0_ + 4, 0:128])
                            osl = (O.t[:, :].rearrange("p (a b) -> p a b", a=4), Dn.t[:, :].rearrange("p (a b) -> p a b", a=4))
                        else:
                            nsl = (numv[:, r0_, nq0 * 128:(nq0 + 4) * 128], denv[:, r0_, nq0 * 128:(nq0 + 4) * 128])
                            osl = (O.t[:, :], Dn.t[:, :])
                        fw.op("dve", lambda v: v.tensor_tensor(out=nsl[0], in0=osl[0], in1=nsl[0], op=ALU.add),
                              [num[h][:, :]], [O[:, :], num[h][:, :]])
                        fw.op("dve", lambda v: v.tensor_tensor(out=nsl[1], in0=osl[1], in1=nsl[1], op=ALU.add),
                              [den[h][:, :]], [Dn[:, :], den[h][:, :]])
            for h in range(4):
                fw.op("dve", lambda v, h=h: v.reciprocal(out=ostg.t[:, :], in_=den[h].t[:, :]), [ostg[:, :]], [den[h][:, :]])
                fw.op("dve", lambda v, h=h: v.tensor_tensor(out=ostg.t[:, :], in0=ostg.t[:, :], in1=num[h].t[:, :], op=ALU.mult),
                      [ostg[:, :]], [ostg[:, :], num[h][:, :]])
                fw.dma("sp", mTs.v(mTs.t[h * 128:(h + 1) * 128, :]), ostg[:, :])
            fw.barrier()
        fw.alloc = stack

        with contextlib.ExitStack() as sub:
            fw.alloc = sub
            dn = Dense(fw, nslot=3, banks=bank, pfx="c_")
            pTs = Buf("pTs", fw.sb("pTs", [128, 2, TP], BF16))
            for p in range(NPASS):
                tok0 = p * TP
                dn.load_x(x1s, tok0)
                dn.load_gb(g2, b2)
                msrc = mTs.t[:, tok0:tok0 + TP].rearrange("(k p) t -> p k t", p=128)
                for k0 in range(0, KD, 4):
                    fw.dma("pool", dn.xT[:, k0:k0 + 4, :], mTs.v(msrc[:, k0:k0 + 4, :]), cont=(k0 > 0))
                fw.dma("pool", pTs[:, :, :], pT.v(pT.t[:, tok0:tok0 + TP].rearrange("(k p) t -> p k t", p=128)))
                dn.scale_x(ALPHA)
                dn.proj_tm(wout, 0, KD, lambda kk, t: dn.xT.t[:, kk, t * 128:(t + 1) * 128], lambda kk: [dn.xT[:, :, :]], 1.0)
                dn.prefetch([(wg2, 0, KD, 0), (wu2, 0, KD, 0)])
                dn.layer_norm()
                dn.load_gb(g3, b3)
                dn.to_feature_major()
                dn.scale_x(ALPHA)
                dn.ffn(wg2, wu2, wd2)
                dn.prefetch([(wpg, 0, KD, 0), (wpp, 0, 2, 0)])
                dn.layer_norm()
                dn.to_feature_major()
                for g in range(D // 512):
                    sg, ag = dn.load_w(wpg, 0, KD, g * 512)
                    sp_, ap_ = dn.load_w(wpp, 0, 2, g * 512)
                    for t in range(dn.NT):
                        bg = dn.bank[(2 * t) % 8]
                        bp = dn.bank[(2 * t + 1) % 8]
                        ts_ = slice(t * 128, (t + 1) * 128)
                        for k in range(KD):
                            fw.op("pe", lambda pe, k=k, bg=bg, ts_=ts_: pe.matmul(
                                bg.t[:, :], lhsT=dn.xT.t[:, k, ts_], rhs=ag[:, k, :], start=(k == 0), stop=(k == KD - 1)),
                                [bg[:, :]], [sg[:, :, :], dn.xT[:, :, :]])
                        for k in range(2):
                            fw.op("pe", lambda pe, k=k, bp=bp, ts_=ts_: pe.matmul(
                                bp.t[:, :], lhsT=pTs.t[:, k, ts_], rhs=ap_[:, k, :], start=(k == 0), stop=(k == 1)),
                                [bp[:, :]], [sp_[:, :, :], pTs[:, :, :]])
                        gs = dn.gsb[t % 2]
                        fw.op("act", lambda a, gs=gs, bg=bg: a.activation(out=gs.t[:, :], in_=bg.t[:, :], func=AF.Sigmoid),
                              [gs[:, :]], [bg[:, :]])
                        fw.op("dve", lambda v, gs=gs, bp=bp: v.tensor_tensor(out=gs.t[:, :], in0=bp.t[:, :], in1=gs.t[:, :], op=ALU.mult),
                              [gs[:, :]], [bp[:, :], gs[:, :]])
                        xs = dn.xin[t].t[:, t, g * 512:(g + 1) * 512]
                        fw.op("dve", lambda v, gs=gs, xs=xs: v.tensor_tensor(out=xs, in0=gs.t[:, :], in1=xs, op=ALU.add),
                              [dn.xin[t][:, t, :]], [gs[:, :], dn.xin[t][:, t, :]])
                dn.store_x_tiles(outs, tok0)
            fw.finish(outs)
            fw.barrier()
        fw.alloc = stack
    return nc


def kernel(**inputs):
    inp = {k: np.asarray(v) for k, v in inputs.items()}
    ca = np.ascontiguousarray
    nc = build_fused()
    xs = inp["x"].reshape(BATCH * SEQ, D)
    pT = ca(inp["p"][0].reshape(BATCH * SEQ, 256).T)
    invf = np.power(np.float32(500000.0), -np.arange(16, dtype=np.float32) * np.float32(2.0 / 32)).astype(np.float32)
    cst = np.zeros((32, 2), np.float32)
    cst[:16, 0] = invf; cst[16:, 0] = invf; cst[:16, 1] = -1.0; cst[16:, 1] = 1.0
    par = np.zeros((NBLK, 128, 8), np.float32)
    for j in range(NBLK):
        sl = slice(j * 128, (j + 1) * 128)
        par[j, :, 0:4] = inp["conv_w"][0][:, sl].T
        par[j, :, 4] = inp["conv_b"][0][sl]; par[j, :, 5] = inp["b_rgate"][0][sl]
        par[j, :, 6] = inp["b_igate"][0][sl]; par[j, :, 7] = inp["lru_lambda"][0][sl]
    common = {
        "cst": cst, "par": par, "wr": ca(inp["w_rgate"][0]), "wi": ca(inp["w_igate"][0]),
        "wg1": ca(inp["ffn1_w_gate"][0]), "wu1": ca(inp["ffn1_w_up"][0]), "wd1": ca(inp["ffn1_w_down"][0]),
        "g1": ca(inp["ln1_g"]), "b1": ca(inp["ln1_b"]), "win": ca(inp["w_in"][0]),
        "wout": ca(inp["w_out"][0]), "g2": ca(inp["ln2_g"]), "b2": ca(inp["ln2_b"]),
        "wg2": ca(inp["ffn2_w_gate"][0]), "wu2": ca(inp["ffn2_w_up"][0]), "wd2": ca(inp["ffn2_w_down"][0]),
        "g3": ca(inp["ln3_g"]), "b3": ca(inp["ln3_b"]), "wpg": ca(inp["w_ple_gate"][0]), "wpp": ca(inp["w_ple_proj"][0]),
    }
    in_maps = []
    for c in range(NCORES):
        b, lr = divmod(c, 4)
        pos = inp["positions"][b].astype(np.int32)
        own = pos[lr * TOK:(lr + 1) * TOK]
        halo = pos[(lr - 1) * TOK:lr * TOK] if lr > 0 else own
        msk = np.zeros((128, 4), np.float32)
        msk[:, :lr] = 1.0
        m = dict(common, x=ca(xs[c * TOK:(c + 1) * TOK]), pT=ca(pT[:, c * TOK:(c + 1) * TOK]),
                 pos=ca(np.concatenate([halo, own])[None, :]), flag=np.full((128, 1), 1.0 if lr > 0 else 0.0, np.float32),
                 msk=msk)
        in_maps.append(m)
    res = run_bass_kernel_spmd(nc, in_maps, core_ids=list(range(NCORES)))
    out = np.concatenate([r["out"] for r in res.results], axis=0)
    return out.reshape(BATCH, SEQ, D).astype(np.float32)
```
